# Optimizing a Trainium2 kernel written in Bass

```python
import jax
import jax.numpy as jnp
from jax import lax
import numpy as np

D_MODEL = 2048
BATCH = 1
SEQ = 8192
DEPTH = 2
DEC_BATCH = 4
DEC_SEQ = 8192
PAST_LEN = 128

MIX_WIDTH = D_MODEL
GROUP_WIDTH = MIX_WIDTH // 4
FNET_GROUPS = 4
FNET_GROUP_DIM = GROUP_WIDTH // FNET_GROUPS
GLA_HEADS = 4
GLA_DK = GROUP_WIDTH // (2 * GLA_HEADS)
GLA_DV = GROUP_WIDTH // GLA_HEADS
GLA_RANK = 16
GLA_TAU = 16.0
GLA_CHUNK = 64
LRU_WIDTH = GROUP_WIDTH
LRU_BLOCKS = 8
LRU_BLOCK_DIM = LRU_WIDTH // LRU_BLOCKS
LRU_C = 8.0
LRU_TAPS = 4
CONF_WIDTH = GROUP_WIDTH
CONF_TAPS = 31
D_FF = 5632
N_SUB = 3
EPS = 1e-6
MIX_IN_SIZES = (GROUP_WIDTH, GLA_HEADS * GLA_DK, GLA_HEADS * GLA_DK, GLA_HEADS * GLA_DV, GLA_HEADS * GLA_DV, 2 * GLA_RANK, LRU_WIDTH, LRU_WIDTH, 2 * CONF_WIDTH)
MIX_IN_WIDTH = sum(MIX_IN_SIZES)

kernel_name = 'hybrid_bidir_encoder_two_groups'


def rms_norm(x, g):
    xf = x.astype(jnp.float32)
    y = xf * lax.rsqrt(jnp.mean(xf * xf, axis=-1, keepdims=True) + EPS)
    return (y * g.astype(jnp.float32)).astype(x.dtype)


def layer_norm(x, g, b):
    xf = x.astype(jnp.float32)
    xc = xf - jnp.mean(xf, axis=-1, keepdims=True)
    y = xc * lax.rsqrt(jnp.mean(xc * xc, axis=-1, keepdims=True) + EPS)
    return (y * g.astype(jnp.float32) + b.astype(jnp.float32)).astype(x.dtype)


def depthwise_conv(x, w, pad_left, pad_right):
    return lax.conv_general_dilated(
        x, w[:, None, :].astype(x.dtype), window_strides=(1,),
        padding=[(pad_left, pad_right)],
        dimension_numbers=('NWC', 'WIO', 'NWC'),
        feature_group_count=x.shape[-1])


def swiglu(h, w_in, w_out):
    up, gate = jnp.split(h @ w_in, 2, axis=-1)
    return (jax.nn.silu(gate) * up) @ w_out


def flip(t):
    return jnp.flip(t, axis=1)


def fourier_mix(u):
    B, S, _ = u.shape
    uf = u.astype(jnp.float32).reshape(B, S, FNET_GROUPS, FNET_GROUP_DIM)
    y = jnp.fft.fft2(uf, axes=(1, 3), norm='ortho').real
    return y.reshape(B, S, GROUP_WIDTH).astype(u.dtype)


def gla_scan(q, k, v, log_a):
    B, S, H, DK = q.shape
    DV = v.shape[-1]
    n = S // GLA_CHUNK

    def to_chunks(t):
        return t.astype(jnp.float32).reshape(B, n, GLA_CHUNK, H, t.shape[-1]).transpose(1, 0, 3, 2, 4)

    qc, kc, vc, gc = to_chunks(q), to_chunks(k), to_chunks(v), to_chunks(log_a)
    mask = jnp.tril(jnp.ones((GLA_CHUNK, GLA_CHUNK), dtype=bool))[:, :, None]

    def step(state, inp):
        qi, ki, vi, gi = inp
        b = jnp.cumsum(gi, axis=2)
        o_inter = jnp.einsum('bhtk,bhkv->bhtv', qi * jnp.exp(b), state)
        diff = b[:, :, :, None, :] - b[:, :, None, :, :]
        decay = jnp.exp(jnp.where(mask, diff, -jnp.inf))
        scores = jnp.einsum('bhtk,bhsk,bhtsk->bhts', qi, ki, decay)
        o_intra = jnp.einsum('bhts,bhsv->bhtv', scores, vi)
        b_last = b[:, :, -1:, :]
        new_state = jnp.exp(b_last[:, :, 0, :])[..., None] * state + jnp.einsum(
            'bhsk,bhsv->bhkv', ki * jnp.exp(b_last - b), vi)
        return new_state, o_inter + o_intra

    state0 = jnp.zeros((B, H, DK, DV), jnp.float32)
    _, out = lax.scan(step, state0, (qc, kc, vc, gc))
    return out.transpose(1, 0, 3, 2, 4).reshape(B, S, H, DV)


def rglru_direction(x, conv_w, conv_b, w_a, b_a, w_x, b_x, lam):
    B, S, W = x.shape
    xc = depthwise_conv(x, conv_w, LRU_TAPS - 1, 0) + conv_b
    xb = xc.reshape(B, S, LRU_BLOCKS, LRU_BLOCK_DIM)
    r = jax.nn.sigmoid((jnp.einsum('bsni,nij->bsnj', xb, w_a).reshape(B, S, W) + b_a).astype(jnp.float32))
    i = jax.nn.sigmoid((jnp.einsum('bsni,nij->bsnj', xb, w_x).reshape(B, S, W) + b_x).astype(jnp.float32))
    log_a = -LRU_C * r * jax.nn.softplus(-lam.astype(jnp.float32))
    a = jnp.exp(log_a)
    u = jnp.sqrt(-jnp.expm1(2.0 * log_a)) * (i * xc.astype(jnp.float32))

    def combine(left, right):
        a_l, h_l = left
        a_r, h_r = right
        return a_l * a_r, a_r * h_l + h_r

    _, hs = lax.associative_scan(combine, (a, u), axis=1)
    return hs.astype(x.dtype)


def hybrid_mixer(h, w_mix_in, gla_w_alpha, gla_b_alpha, gla_norm_g,
                 lru_conv_w, lru_conv_b, lru_w_a, lru_b_a, lru_w_x, lru_b_x, lru_lambda,
                 conf_dw_w, conf_dw_b, conf_ln_g, conf_ln_b, w_mix_out):
    B, S, _ = h.shape
    offsets = [sum(MIX_IN_SIZES[:j + 1]) for j in range(len(MIX_IN_SIZES) - 1)]
    f_in, q, k, v, o_gate, a_lr, r_in, r_gate, c_in = jnp.split(h @ w_mix_in, offsets, axis=-1)

    y_f = fourier_mix(f_in)

    q = q.reshape(B, S, GLA_HEADS, GLA_DK) * (GLA_DK ** -0.5)
    k = k.reshape(B, S, GLA_HEADS, GLA_DK)
    v = v.reshape(B, S, GLA_HEADS, GLA_DV)
    z = jnp.einsum('bsdr,drk->bsdk', a_lr.reshape(B, S, 2, GLA_RANK), gla_w_alpha) + gla_b_alpha
    log_a = (jax.nn.log_sigmoid(z.astype(jnp.float32)) / GLA_TAU).reshape(B, S, 2, GLA_HEADS, GLA_DK)
    o_fwd = gla_scan(q, k, v, log_a[:, :, 0])
    o_bwd = flip(gla_scan(flip(q), flip(k), flip(v), flip(log_a[:, :, 1])))
    o = rms_norm((o_fwd + o_bwd).astype(h.dtype), gla_norm_g) * jax.nn.silu(o_gate.reshape(B, S, GLA_HEADS, GLA_DV))
    y_g = o.reshape(B, S, GROUP_WIDTH)

    h_fwd = rglru_direction(r_in, lru_conv_w[0], lru_conv_b[0], lru_w_a[0], lru_b_a[0],
                            lru_w_x[0], lru_b_x[0], lru_lambda[0])
    h_bwd = flip(rglru_direction(flip(r_in), lru_conv_w[1], lru_conv_b[1], lru_w_a[1], lru_b_a[1],
                                 lru_w_x[1], lru_b_x[1], lru_lambda[1]))
    y_r = (h_fwd + h_bwd) * jax.nn.gelu(r_gate)

    c_val, c_g = jnp.split(c_in, 2, axis=-1)
    u = c_val * jax.nn.sigmoid(c_g)
    u = depthwise_conv(u, conf_dw_w, CONF_TAPS // 2, CONF_TAPS // 2) + conf_dw_b
    y_c = jax.nn.silu(layer_norm(u, conf_ln_g, conf_ln_b))

    return jnp.concatenate([y_f, y_g, y_r, y_c], axis=-1) @ w_mix_out


def encoder_layer(x, c, w_ada, b_ada, g_pre, g_post, ffn1_w_in, ffn1_w_out, ffn2_w_in, ffn2_w_out,
                  w_mix_in, gla_w_alpha, gla_b_alpha, gla_norm_g,
                  lru_conv_w, lru_conv_b, lru_w_a, lru_b_a, lru_w_x, lru_b_x, lru_lambda,
                  conf_dw_w, conf_dw_b, conf_ln_g, conf_ln_b, w_mix_out):
    B = x.shape[0]
    mod = (jax.nn.silu(c) @ w_ada + b_ada).reshape(B, N_SUB, 3, D_MODEL)[:, :, :, None, :]

    def pre(t, j):
        return rms_norm(t, g_pre[j]) * (1.0 + mod[:, j, 1]) + mod[:, j, 0]

    def post(t, out, j, w):
        return t + w * mod[:, j, 2] * rms_norm(out, g_post[j])

    x = post(x, swiglu(pre(x, 0), ffn1_w_in, ffn1_w_out), 0, 0.5)
    x = post(x, hybrid_mixer(pre(x, 1), w_mix_in, gla_w_alpha, gla_b_alpha, gla_norm_g,
                             lru_conv_w, lru_conv_b, lru_w_a, lru_b_a, lru_w_x, lru_b_x, lru_lambda,
                             conf_dw_w, conf_dw_b, conf_ln_g, conf_ln_b, w_mix_out), 1, 1.0)
    x = post(x, swiglu(pre(x, 2), ffn2_w_in, ffn2_w_out), 2, 0.5)
    return x


def setup_inputs(seed: int = 0) -> dict:
    key = jax.random.key(seed)
    L, D = DEPTH, D_MODEL
    ks = jax.random.split(key, 28)

    def nrm(j, shape, scale):
        return scale * jax.random.normal(ks[j], shape, jnp.float32)

    a_c = jax.random.uniform(ks[22], (L, 2, LRU_WIDTH), jnp.float32, 0.9, 0.999)
    sig = a_c ** (1.0 / LRU_C)
    lru_lambda = jnp.log(sig) - jnp.log1p(-sig)
    return {
        'x_prompt': nrm(0, (BATCH, SEQ, D), 1.0),
        'x_sample': nrm(1, (DEC_BATCH, DEC_SEQ, D), 1.0),
        'c_prompt': nrm(2, (BATCH, D), 1.0),
        'c_sample': nrm(3, (DEC_BATCH, D), 1.0),
        'w_ada': nrm(4, (L, D, N_SUB * 3 * D), 0.5 * D ** -0.5),
        'b_ada': nrm(5, (L, N_SUB * 3 * D), 0.01),
        'g_pre': 1.0 + nrm(6, (L, N_SUB, D), 0.02),
        'g_post': 1.0 + nrm(7, (L, N_SUB, D), 0.02),
        'ffn1_w_in': nrm(8, (L, D, 2 * D_FF), D ** -0.5),
        'ffn1_w_out': nrm(9, (L, D_FF, D), D_FF ** -0.5),
        'ffn2_w_in': nrm(10, (L, D, 2 * D_FF), D ** -0.5),
        'ffn2_w_out': nrm(11, (L, D_FF, D), D_FF ** -0.5),
        'w_mix_in': nrm(12, (L, D, MIX_IN_WIDTH), D ** -0.5),
        'gla_w_alpha': nrm(13, (L, 2, GLA_RANK, GLA_HEADS * GLA_DK), GLA_RANK ** -0.5),
        'gla_b_alpha': nrm(14, (L, 2, GLA_HEADS * GLA_DK), 0.1),
        'gla_norm_g': 1.0 + nrm(15, (L, GLA_DV), 0.02),
        'lru_conv_w': nrm(16, (L, 2, LRU_TAPS, LRU_WIDTH), LRU_TAPS ** -0.5),
        'lru_conv_b': nrm(17, (L, 2, LRU_WIDTH), 0.01),
        'lru_w_a': nrm(18, (L, 2, LRU_BLOCKS, LRU_BLOCK_DIM, LRU_BLOCK_DIM), LRU_BLOCK_DIM ** -0.5),
        'lru_b_a': nrm(19, (L, 2, LRU_WIDTH), 0.01),
        'lru_w_x': nrm(20, (L, 2, LRU_BLOCKS, LRU_BLOCK_DIM, LRU_BLOCK_DIM), LRU_BLOCK_DIM ** -0.5),
        'lru_b_x': nrm(21, (L, 2, LRU_WIDTH), 0.01),
        'lru_lambda': lru_lambda,
        'conf_dw_w': nrm(23, (L, CONF_TAPS, CONF_WIDTH), CONF_TAPS ** -0.5),
        'conf_dw_b': nrm(24, (L, CONF_WIDTH), 0.01),
        'conf_ln_g': 1.0 + nrm(25, (L, CONF_WIDTH), 0.02),
        'conf_ln_b': nrm(26, (L, CONF_WIDTH), 0.01),
        'w_mix_out': nrm(27, (L, MIX_WIDTH, D), MIX_WIDTH ** -0.5),
    }


def reference(x_prompt, x_sample, c_prompt, c_sample, w_ada, b_ada, g_pre, g_post,
              ffn1_w_in, ffn1_w_out, ffn2_w_in, ffn2_w_out, w_mix_in,
              gla_w_alpha, gla_b_alpha, gla_norm_g,
              lru_conv_w, lru_conv_b, lru_w_a, lru_b_a, lru_w_x, lru_b_x, lru_lambda,
              conf_dw_w, conf_dw_b, conf_ln_g, conf_ln_b, w_mix_out):
    def run(x, c):
        for l in range(DEPTH):
            x = encoder_layer(x, c, w_ada[l], b_ada[l], g_pre[l], g_post[l],
                              ffn1_w_in[l], ffn1_w_out[l], ffn2_w_in[l], ffn2_w_out[l],
                              w_mix_in[l], gla_w_alpha[l], gla_b_alpha[l], gla_norm_g[l],
                              lru_conv_w[l], lru_conv_b[l], lru_w_a[l], lru_b_a[l],
                              lru_w_x[l], lru_b_x[l], lru_lambda[l],
                              conf_dw_w[l], conf_dw_b[l], conf_ln_g[l], conf_ln_b[l], w_mix_out[l])
        return x

    y_prompt = run(x_prompt, c_prompt)
    y_sample = run(x_sample, c_sample)
    return (y_prompt, y_sample)
```

```python
import math
from contextlib import ExitStack
import numpy as np
import concourse.bass as bass
import concourse.mybir as mybir
from concourse.bass_utils import run_bass_kernel_spmd

F32 = mybir.dt.float32
BF16 = mybir.dt.bfloat16
AF = mybir.ActivationFunctionType
ALU = mybir.AluOpType

D = 2048
KC = 16
L = 2
EPS = 1e-6
T = 512
NCH_FM = 29

CH_U, CH_Q, CH_K, CH_OG, CH_RIN, CH_RG, CH_C, CH_A = 0, 4, 6, 8, 12, 16, 20, 28


class Op:
    __slots__ = ("eng", "fn", "deps", "is_dma", "key", "dval", "signal", "ticket")

    def __init__(self, eng, fn, is_dma=False):
        self.eng = eng
        self.fn = fn
        self.deps = []
        self.is_dma = is_dma
        self.key = None
        self.dval = 0
        self.signal = False
        self.ticket = 0


class Res:
    __slots__ = ("w", "r")

    def __init__(self):
        self.w = None
        self.r = {}


ENGS = ("pe", "act", "dve", "pool", "sp")
GLOB = {}


class Phase:
    def __init__(self, nc, name):
        self.nc = nc
        self.name = name
        self.ops = {e: [] for e in ENGS}
        self.ctx = ExitStack()
        self.dma_cnt = {}
        self.n = 0

    def sb(self, name, shape, dt):
        return self.ctx.enter_context(self.nc.sbuf_tensor(f"{self.name}_{name}", shape, dt))

    def ps(self, name, shape=(128, 512), dt=F32):
        return self.ctx.enter_context(self.nc.psum_tensor(f"{self.name}_{name}", list(shape), dt))

    def _mk(self, eng, fn, rd, wr, is_dma=False, strict=False):
        op = Op(eng, fn, is_dma)
        deps = []
        for r in rd:
            if r.w is not None:
                deps.append(r.w)
        for w in wr:
            if w.w is not None:
                deps.append(w.w)
            deps.extend(w.r.values())
        seen = set()
        for d in deps:
            if id(d) in seen or d is op:
                continue
            seen.add(id(d))
            if d.is_dma or d.eng != eng or is_dma or strict:
                op.deps.append(d)
                if not d.is_dma:
                    d.signal = True
        for r in rd:
            if is_dma:
                r.r[("dma", self.n)] = op
            else:
                r.r[eng] = op
        for w in wr:
            w.w = op
            w.r = {}
        self.n += 1
        self.ops[eng].append(op)
        return op

    def op(self, eng, fn, rd=(), wr=(), strict=False):
        return self._mk(eng, fn, rd, wr, strict=strict)

    def dma(self, eng, out, in_, key, rd=(), wr=(), **kw):
        def fn(e):
            return e.dma_start(out=out, in_=in_, **kw)
        op = self._mk(eng, fn, rd, wr, is_dma=True)
        c = self.dma_cnt.get(key, 0) + 16
        self.dma_cnt[key] = c
        op.key = key
        op.dval = c
        return op

    def run(self):
        nc = self.nc
        if DBG.get("only") and self.name not in DBG["only"]:
            self.ctx.close()
            if DBG["stop"] == self.name:
                raise _Stop()
            return
        G = GLOB[id(nc)]
        engs = {"pe": "tensor", "act": "scalar", "dve": "vector", "pool": "gpsimd", "sp": "sync"}
        for e in ENGS:
            if e not in G["esem"]:
                G["esem"][e] = G["st"].enter_context(nc.semaphore(f"s_{e}"))
                G["ecnt"][e] = 0
        for k in self.dma_cnt:
            if k not in G["dsem"]:
                G["dsem"][k] = G["st"].enter_context(nc.semaphore(f"d_{k}"))
                G["dcnt"][k] = 0
        esem, dsem = G["esem"], G["dsem"]
        ebase = dict(G["ecnt"])
        dbase = dict(G["dcnt"])
        for e in ENGS:
            t = ebase[e]
            for op in self.ops[e]:
                if op.signal and not op.is_dma:
                    t += 1
                    op.ticket = t
            G["ecnt"][e] = t
        for k, c in self.dma_cnt.items():
            G["dcnt"][k] = dbase[k] + c
        with ExitStack() as st:
            block = st.enter_context(nc.Block())

            def emit(e, eng):
                waited = {}
                for op in self.ops[e]:
                    for d in op.deps:
                        if d.is_dma:
                            k, v, s = ("d", d.key), dbase[d.key] + d.dval, dsem[d.key]
                        else:
                            k, v, s = ("e", d.eng), d.ticket, esem[d.eng]
                        if waited.get(k, 0) >= v:
                            continue
                        waited[k] = v
                        eng.wait_ge(s, v)
                    ins = op.fn(eng)
                    if op.is_dma:
                        ins.then_inc(dsem[op.key], 16)
                    elif op.signal:
                        ins.then_inc(esem[e], 1)
                if e == "sp":
                    for k, c in self.dma_cnt.items():
                        if waited.get(("d", k), 0) < dbase[k] + c:
                            eng.wait_ge(dsem[k], dbase[k] + c)

            for e in ENGS:
                getattr(block, engs[e])(lambda eng, e=e: emit(e, eng))
        self.ctx.close()
        if DBG["stop"] == self.name:
            raise _Stop()


DBG = {"stop": None, "outs": ()}


class _Stop(Exception):
    pass


def build_program(S, DFF):
    nc = bass.Bass("TRN2", target_bir_lowering=False)
    GLOB[id(nc)] = {"st": ExitStack(), "esem": {}, "dsem": {}, "ecnt": {}, "dcnt": {}}
    try:
        _build(nc, S, DFF)
    except _Stop:
        pass
    GLOB[id(nc)]["st"].close()
    return nc


def _build(nc, S, DFF):
    NT = S // T
    HC = DFF // 128
    HG = HC // 2
    N2 = S // 128
    CHP = 128 // N2
    NB = S // 128
    TT = min(2048, S)
    NTT = S // TT
    def din(name, shape, dt=F32):
        return nc.dram_tensor(name, list(shape), dt, kind="ExternalInput").ap()

    def dscr(name, shape, dt=F32):
        kind = "ExternalOutput" if name in DBG["outs"] else "Internal"
        return nc.dram_tensor(name, list(shape), dt, kind=kind).ap()

    xT_in = din("xT", [D, S])
    cT_in = din("cT", [128, KC])
    wada = din("wada", [L, 36, 128, KC, 512])
    bada = din("bada", [128, L * 144])
    gpre = din("gpre", [128, L * 48])
    gpost = din("gpost", [128, L * 48])
    win_f = [din("w1in", [L, HC, 128, KC, 256]), din("w2in", [L, HC, 128, KC, 256])]
    wout_f = [din("w1out", [L, KC, 128, HC, 128]), din("w2out", [L, KC, 128, HC, 128])]
    wmi_f = din("wmi", [L, NCH_FM, 128, KC, 128])
    wmt_f = din("wmt", [L, 128, KC, 768])
    wmo_f = din("wmo", [L, KC, 128, KC, 128])
    walpha = din("walpha", [64, L * 2 * 256])
    balpha = din("balpha", [64, L * 2 * 256])
    gnorm = din("gnorm", [128, L])
    lcw = din("lcw", [128, L * 2 * 4 * 4])
    lcb = din("lcb", [128, L * 2 * 4])
    lwa = din("lwa", [L * 2 * 4, 128, 128])
    lwx = din("lwx", [L * 2 * 4, 128, 128])
    lba = din("lba", [128, L * 2 * 4])
    lbx = din("lbx", [128, L * 2 * 4])
    llam = din("llam", [128, L * 2 * 4])
    cfw = din("cfw", [128, L * 4 * 31])
    cfd = din("cfd", [L * 4, 128, 31, 128])
    cfb = din("cfb", [128, L * 4])
    clg = din("clg", [128, L * 4])
    clb = din("clb", [128, L * 4])
    c_ident = din("c_ident", [128, 128])
    c_tri = din("c_tri", [4, 128, 128])
    c_mask = din("c_mask", [2, 128, 128])
    c_cs128 = din("c_cs128", [128, 256])
    c_sa = din("c_sa", [2, 128, 256])
    c_tw = din("c_tw", [2, 128, 128])
    c_bd = din("c_bd", [2, 128, 128])
    yT_out = nc.dram_tensor("yT", [D, S], F32, kind="ExternalOutput").ap()

    win_b = [dscr("w1in_b", [L, HC, 128, KC, 256], BF16), dscr("w2in_b", [L, HC, 128, KC, 256], BF16)]
    wout_b = [dscr("w1out_b", [L, KC, 128, HC, 128], BF16), dscr("w2out_b", [L, KC, 128, HC, 128], BF16)]
    wmi_b = dscr("wmi_b", [L, NCH_FM, 128, KC, 128], BF16)
    wmt_b = dscr("wmt_b", [L, 128, KC, 768], BF16)
    wmo_b = dscr("wmo_b", [L, KC, 128, KC, 128], BF16)
    mods_d = dscr("mods", [128, L * 144])
    xA = dscr("xA", [D, S])
    xB = dscr("xB", [D, S])
    xC = dscr("xC", [D, S])
    UT = dscr("UT", [512, S], BF16)
    QT = dscr("QT", [256, S])
    KT = dscr("KT", [256, S])
    SOG = dscr("SOG", [512, S])
    RIN = dscr("RIN", [512, S + 6])
    GR = dscr("GR", [512, S])
    UC = dscr("UC", [512, S + 30])
    AT = dscr("AT", [64, S])
    KTOK = dscr("KTOK", [S, 256])
    VTOK = dscr("VTOK", [S, 512], BF16)
    OF = dscr("OF", [S, 512])
    HF = dscr("HF", [512, S])
    Y = dscr("Y", [D, S], BF16)
    DBGT = dscr("DBGT", [8, 128, TT]) if "DBGT" in DBG["outs"] else None
    DBGC = dscr("DBGC", [128, 48]) if "DBGT" in DBG["outs"] else None

    ph = Phase(nc, "cvt")
    k = 0
    r_cvt = [Res() for _ in range(4)]

    def cvt(ph, dst, src, rows_per=8):
        nonlocal k
        R = src.shape[0]
        for r0 in range(0, R, rows_per):
            r1 = min(R, r0 + rows_per)
            ph.dma("pool", dst[r0:r1], src[r0:r1], key=f"c{k % 4}", wr=[r_cvt[k % 4]], max_dma_last_dim=8192)
            k += 1

    def cvt_layer(ph, l):
        for i in range(2):
            cvt(ph, win_b[i][l].rearrange("g p k n -> (g p) (k n)"), win_f[i][l].rearrange("g p k n -> (g p) (k n)"), 256)
            cvt(ph, wout_b[i][l].rearrange("g p k n -> (g p) (k n)"), wout_f[i][l].rearrange("g p k n -> (g p) (k n)"), 256)
        cvt(ph, wmi_b[l].rearrange("g p k n -> (g p) (k n)"), wmi_f[l].rearrange("g p k n -> (g p) (k n)"), 512)
        cvt(ph, wmt_b[l].rearrange("p k n -> p (k n)"), wmt_f[l].rearrange("p k n -> p (k n)"), 128)
        cvt(ph, wmo_b[l].rearrange("g p k n -> (g p) (k n)"), wmo_f[l].rearrange("g p k n -> (g p) (k n)"), 512)

    for l in range(L):
        cvt_layer(ph, l)
    zt = ph.sb("zt", [128, 32], F32)
    zr = Res()
    ph.op("dve", lambda e: e.memset(zt[:], 0.0), wr=[zr])
    for cc in range(4):
        ph.dma("sp", RIN[cc * 128:(cc + 1) * 128, 0:3], zt[:, 0:3], key="z", rd=[zr])
        ph.dma("sp", RIN[cc * 128:(cc + 1) * 128, S + 3:S + 6], zt[:, 0:3], key="z", rd=[zr])
        ph.dma("sp", UC[cc * 128:(cc + 1) * 128, 0:15], zt[:, 0:15], key="z", rd=[zr])
        ph.dma("sp", UC[cc * 128:(cc + 1) * 128, S + 15:S + 30], zt[:, 0:15], key="z", rd=[zr])

    ct = ph.sb("ct", [128, KC], F32)
    sc = ph.sb("sc", [128, KC], F32)
    bad = ph.sb("bad", [128, L * 144], F32)
    gpr = ph.sb("gpr", [128, L * 48], F32)
    gpo = ph.sb("gpo", [128, L * 48], F32)
    modr = ph.sb("modr", [128, 144], F32)
    modo = ph.sb("modo", [128, L * 144], F32)
    wsl = [ph.sb(f"w{i}", [128, KC, 512], F32) for i in range(2)]
    pm = ph.ps("pm")
    r_ct, r_sc, r_small, r_pm, r_modr, r_modo = Res(), Res(), Res(), Res(), Res(), Res()
    r_w = [Res(), Res()]
    ph.dma("sp", ct[:], cT_in[:, :], key="ld0", wr=[r_ct])
    ph.dma("sp", bad[:], bada[:, :], key="ld1", wr=[r_small])
    ph.dma("sp", gpr[:], gpre[:, :], key="ld1", wr=[r_small])
    ph.dma("sp", gpo[:], gpost[:, :], key="ld1", wr=[r_small])
    ph.op("act", lambda e: e.activation(out=sc[:], in_=ct[:], func=AF.Silu), rd=[r_ct], wr=[r_sc])
    it = 0
    for l in range(L):
        for cg in range(36):
            s = it % 2
            it += 1
            ph.dma("sp", wsl[s][:], wada[l, cg], key=f"w{s}", wr=[r_w[s]])
            for sub in range(4):
                q = cg * 4 + sub
                for kc in range(KC):
                    ph.op("pe", lambda e, s=s, sub=sub, kc=kc, q=q: e.matmul(
                        pm[:, q:q + 1], lhsT=wsl[s][:, kc, sub * 128:(sub + 1) * 128], rhs=sc[:, kc:kc + 1],
                        start=(kc == 0), stop=(kc == KC - 1)), rd=[r_w[s], r_sc], wr=[r_pm])
        ph.op("dve", lambda e, l=l: e.tensor_tensor(out=modr[:], in0=pm[:, 0:144], in1=bad[:, l * 144:(l + 1) * 144],
                                                    op=ALU.add), rd=[r_pm, r_small], wr=[r_modr])
        for j in range(3):
            wj = 1.0 if j == 1 else 0.5
            o = l * 144
            sh = modr[:, (j * 3 + 0) * 16:(j * 3 + 1) * 16]
            scl = modr[:, (j * 3 + 1) * 16:(j * 3 + 2) * 16]
            gt = modr[:, (j * 3 + 2) * 16:(j * 3 + 3) * 16]
            ph.op("dve", lambda e, o=o, j=j, scl=scl, l=l: e.scalar_tensor_tensor(
                out=modo[:, o + j * 16:o + (j + 1) * 16], in0=scl, scalar=1.0, in1=gpr[:, l * 48 + j * 16:l * 48 + (j + 1) * 16],
                op0=ALU.add, op1=ALU.mult), rd=[r_modr, r_small], wr=[r_modo], strict=True)
            ph.op("dve", lambda e, o=o, j=j, sh=sh: e.tensor_copy(out=modo[:, o + 48 + j * 16:o + 48 + (j + 1) * 16], in_=sh),
                  rd=[r_modr], wr=[r_modo], strict=True)
            ph.op("dve", lambda e, o=o, j=j, gt=gt, wj=wj, l=l: e.scalar_tensor_tensor(
                out=modo[:, o + 96 + j * 16:o + 96 + (j + 1) * 16], in0=gt, scalar=wj, in1=gpo[:, l * 48 + j * 16:l * 48 + (j + 1) * 16],
                op0=ALU.mult, op1=ALU.mult), rd=[r_modr, r_small], wr=[r_modo], strict=True)
    ph.dma("sp", mods_d[:, :], modo[:], key="st", rd=[r_modo])
    ph.run()

    class TileCtx:
        def __init__(self, ph, with_xT=True, with_ring=True):
            self.ph = ph
            self.xT = ph.sb("xT", [128, KC, T], F32) if with_xT else None
            self.xo = ph.sb("xo", [128, KC, T], F32)
            self.mods = ph.sb("mods", [128, L * 144], F32)
            self.ones = ph.sb("ones", [128, 128], BF16)
            self.eps = ph.sb("eps", [128, 1], F32)
            self.sq = [ph.sb(f"sq{i}", [128, T], BF16) for i in range(2)]
            self.rstd = ph.sb("rstd", [128, T], F32)
            self.tmp = [ph.sb(f"tmp{i}", [128, T], F32) for i in range(2)]
            self.ring = [ph.sb(f"ring{i}", [128, T], F32) for i in range(4)] if with_ring else None
            self.pst = ph.ps("pst")
            self.r_x = [Res() for _ in range(KC)]
            self.r_xo = Res()
            self.r_mods = Res()
            self.r_const = Res()
            self.r_sq = [Res(), Res()]
            self.r_tmp = [Res(), Res()]
            self.r_ring = [Res() for _ in range(4)]
            self.r_rstd = Res()
            self.r_pst = Res()
            self.nring = 0
            ph.dma("sp", self.mods[:], mods_d[:, :], key="cst", wr=[self.r_mods])
            ph.op("dve", lambda e: e.memset(self.ones[:], 1.0), wr=[self.r_const])
            ph.op("dve", lambda e: e.memset(self.eps[:], EPS), wr=[self.r_const])

    def rms_stats(ph, tc, src_chunks, src_res, n_feat, cnt):
        n = len(src_chunks)
        for i, (ap, rr) in enumerate(zip(src_chunks, src_res)):
            b = cnt[0] % 2
            cnt[0] += 1
            ph.op("act", lambda e, ap=ap, b=b: e.activation(out=tc.sq[b][:], in_=ap, func=AF.Square),
                  rd=[rr], wr=[tc.r_sq[b]])
            ph.op("pe", lambda e, b=b, i=i: e.matmul(tc.pst[:], lhsT=tc.ones[:], rhs=tc.sq[b][:], start=(i == 0),
                                                     stop=(i == n - 1)), rd=[tc.r_sq[b], tc.r_const], wr=[tc.r_pst])
        ph.op("act", lambda e: e.activation(out=tc.rstd[:], in_=tc.pst[:], func=AF.Sqrt, scale=1.0 / n_feat, bias=tc.eps[:]),
              rd=[tc.r_pst, tc.r_const], wr=[tc.r_rstd])
        ph.op("dve", lambda e: e.reciprocal(out=tc.rstd[:], in_=tc.rstd[:]), rd=[tc.r_rstd], wr=[tc.r_rstd])

    def pre_norm(ph, tc, xn, r_xn, l, j, cnt, src=None, src_res=None):
        if src is None:
            src, src_res = tc.xT, tc.r_x
        rms_stats(ph, tc, [src[:, kc, :] for kc in range(KC)], src_res, float(D), cnt)
        o = l * 144
        for kc in range(KC):
            b = kc % 2
            ph.op("dve", lambda e, kc=kc, b=b: e.tensor_tensor(out=tc.tmp[b][:], in0=src[:, kc, :], in1=tc.rstd[:], op=ALU.mult),
                  rd=[src_res[kc], tc.r_rstd], wr=[tc.r_tmp[b]])
            ph.op("act", lambda e, kc=kc, b=b: e.activation(
                out=xn[:, kc, :], in_=tc.tmp[b][:], func=AF.Identity,
                scale=tc.mods[:, o + j * 16 + kc:o + j * 16 + kc + 1], bias=tc.mods[:, o + 48 + j * 16 + kc:o + 48 + j * 16 + kc + 1]),
                rd=[tc.r_tmp[b], tc.r_mods], wr=[r_xn])

    def post_stream(ph, tc, l, j, cnt, x_dram, t0, dres=None):
        rms_stats(ph, tc, [tc.xo[:, kc, :] for kc in range(KC)], [tc.r_xo] * KC, float(D), cnt)
        o = l * 144
        for kc in range(KC):
            sl = tc.nring % 4
            tc.nring += 1
            ph.dma("sp", tc.ring[sl][:], x_dram[kc * 128:(kc + 1) * 128, t0:t0 + T], key=f"xr{sl}", rd=([dres] if dres is not None else []),
                   wr=[tc.r_ring[sl]])
            ph.op("dve", lambda e, kc=kc: e.tensor_tensor(out=tc.xo[:, kc, :], in0=tc.xo[:, kc, :], in1=tc.rstd[:], op=ALU.mult),
                  rd=[tc.r_xo, tc.r_rstd], wr=[tc.r_xo])
            ph.op("dve", lambda e, kc=kc, sl=sl: e.scalar_tensor_tensor(
                out=tc.xo[:, kc, :], in0=tc.xo[:, kc, :], scalar=tc.mods[:, o + 96 + j * 16 + kc:o + 96 + j * 16 + kc + 1],
                in1=tc.ring[sl][:], op0=ALU.mult, op1=ALU.add), rd=[tc.r_xo, tc.r_ring[sl], tc.r_mods], wr=[tc.r_xo])

    def store_xo(ph, tc, dst, t0, dres=None):
        ph.dma("sp", dst[:, t0:t0 + T].rearrange("(c p) t -> p c t", p=128), tc.xo[:], key="xs", rd=[tc.r_xo],
               wr=([dres] if dres is not None else []))

    def load_x(ph, tc, src, t0):
        ph.dma("sp", tc.xT[:], src[:, t0:t0 + T].rearrange("(c p) t -> p c t", p=128), key="xl", wr=list(tc.r_x))

    class FFN:
        def __init__(self, ph, tc):
            self.ph, self.tc = ph, tc
            self.xn = ph.sb("xn", [128, KC, T], BF16)
            self.hT = ph.sb("hT", [128, HC, T], BF16)
            self.win = [ph.sb(f"win{i}", [128, KC, 256], BF16) for i in range(2)]
            self.wout = [ph.sb(f"wout{i}", [128, HC, 128], BF16) for i in range(2)]
            self.sg = [ph.sb(f"sg{i}", [128, T], F32) for i in range(2)]
            self.pu = [ph.ps(f"pu{i}") for i in range(2)]
            self.pg = [ph.ps(f"pg{i}") for i in range(2)]
            self.po = [ph.ps(f"po{i}") for i in range(2)]
            self.r_xn = Res()
            self.r_h = [Res() for _ in range(HC)]
            self.r_win = [Res(), Res()]
            self.r_wout = [Res(), Res()]
            self.r_sg = [Res(), Res()]
            self.r_pu = [Res(), Res()]
            self.r_pg = [Res(), Res()]
            self.r_po = [Res(), Res()]
            self.nin = 0
            self.nout = 0
            self.nj = 0

        def tile_in(self, l, which, hook=None):
            ph = self.ph
            for hc in range(HC):
                s = self.nin % 2
                self.nin += 1
                ph.dma("sp", self.win[s][:], win_b[which][l, hc], key=f"win{s}", wr=[self.r_win[s]])
                b = self.nj % 2
                self.nj += 1
                for kc in range(KC):
                    ph.op("pe", lambda e, s=s, kc=kc, b=b: e.matmul(
                        self.pu[b][:], lhsT=self.win[s][:, kc, 0:128], rhs=self.xn[:, kc, :],
                        start=(kc == 0), stop=(kc == KC - 1)), rd=[self.r_win[s], self.r_xn], wr=[self.r_pu[b]])
                for kc in range(KC):
                    ph.op("pe", lambda e, s=s, kc=kc, b=b: e.matmul(
                        self.pg[b][:], lhsT=self.win[s][:, kc, 128:256], rhs=self.xn[:, kc, :],
                        start=(kc == 0), stop=(kc == KC - 1)), rd=[self.r_win[s], self.r_xn], wr=[self.r_pg[b]])
                ph.op("act", lambda e, b=b: e.activation(out=self.sg[b][:], in_=self.pg[b][:], func=AF.Silu),
                      rd=[self.r_pg[b]], wr=[self.r_sg[b]])
                ph.op("dve", lambda e, b=b, hc=hc: e.tensor_tensor(out=self.hT[:, hc, :], in0=self.pu[b][:], in1=self.sg[b][:],
                                                                  op=ALU.mult),
                      rd=[self.r_pu[b], self.r_sg[b]], wr=[self.r_h[hc]])
                if hook is not None and hc == 2:
                    hook()

        def tile_out(self, l, which, hook=None):
            ph, tc = self.ph, self.tc
            for dc in range(KC):
                s = self.nout % 2
                self.nout += 1
                ph.dma("sp", self.wout[s][:], wout_b[which][l, dc], key=f"wout{s}", wr=[self.r_wout[s]])
                for hc in range(HC):
                    ph.op("pe", lambda e, s=s, hc=hc: e.matmul(
                        self.po[s][:], lhsT=self.wout[s][:, hc, :], rhs=self.hT[:, hc, :], start=(hc == 0), stop=(hc == HC - 1)),
                        rd=[self.r_wout[s], self.r_h[hc]], wr=[self.r_po[s]])
                ph.op("act", lambda e, s=s, dc=dc: e.activation(out=tc.xo[:, dc, :], in_=self.po[s][:], func=AF.Copy),
                      rd=[self.r_po[s]], wr=[tc.r_xo])
                if hook is not None and dc == 3:
                    hook()

    for l in range(L):
        x_src = xT_in if l == 0 else xB
        ph = Phase(nc, f"f1_{l}")
        tc = TileCtx(ph)
        ffn = FFN(ph, tc)
        cnt = [0]
        load_x(ph, tc, x_src, 0)
        pre_norm(ph, tc, ffn.xn, ffn.r_xn, l, 0, cnt)
        for ti in range(NT):
            nxt = ti + 1 < NT
            ffn.tile_in(l, 0, hook=(lambda ti=ti: load_x(ph, tc, x_src, (ti + 1) * T)) if nxt else None)
            ffn.tile_out(l, 0, hook=(lambda: pre_norm(ph, tc, ffn.xn, ffn.r_xn, l, 0, cnt)) if nxt else None)
            post_stream(ph, tc, l, 0, cnt, x_src, ti * T)
            store_xo(ph, tc, xA, ti * T)
        ph.run()

        ph = Phase(nc, f"mi_{l}")
        tc = TileCtx(ph, with_ring=False)
        xn = ph.sb("xn", [128, KC, T], BF16)
        r_xn = Res()
        wfm = [ph.sb(f"wfm{i}", [128, 4, KC, 128], BF16) for i in range(2)]
        r_wfm = [Res(), Res()]
        wtk = ph.sb("wtk", [128, KC, 768], BF16)
        r_wtk = Res()
        stg_u = ph.sb("stg_u", [128, 4, T], BF16)
        stg_f = [ph.sb(f"stg_f{i}", [128, 4, T], F32) for i in range(2)]
        sig = ph.sb("sig", [128, T], F32)
        stg_a = ph.sb("stg_a", [64, T], F32)
        stg_k = ph.sb("stg_k", [128, 4, 256], F32)
        stg_v = ph.sb("stg_v", [128, 4, 512], BF16)
        pp = [ph.ps(f"pp{i}") for i in range(3)]
        pk = ph.ps("pk")
        pv = ph.ps("pv")
        r_pp = [Res() for _ in range(3)]
        r_pk, r_pv = Res(), Res()
        r_su, r_sf, r_sig, r_sa, r_sk, r_sv = Res(), [Res(), Res()], Res(), Res(), Res(), Res()
        ph.dma("sp", wtk[:], wmt_b[l], key="wtk", wr=[r_wtk])
        cnt = [0]
        nw = 0
        npp = 0
        nsf = 0
        for ti in range(NT):
            t0 = ti * T
            load_x(ph, tc, xA, t0)
            pre_norm(ph, tc, xn, r_xn, l, 1, cnt)
            cur_sf = None
            for ch in range(NCH_FM):
                if ch % 4 == 0:
                    s = nw % 2
                    nw += 1
                    n_in = min(4, NCH_FM - ch)
                    ph.dma("sp", wfm[s][:, 0:n_in], wmi_b[l, ch:ch + n_in].rearrange("g p k n -> p g k n"), key=f"wfm{s}",
                           wr=[r_wfm[s]])
                b = npp % 3
                npp += 1
                M = 64 if ch == CH_A else 128
                for kc in range(KC):
                    ph.op("pe", lambda e, s=s, ch=ch, kc=kc, b=b, M=M: e.matmul(
                        pp[b][0:M, :], lhsT=wfm[s][:, ch % 4, kc, 0:M], rhs=xn[:, kc, :], start=(kc == 0), stop=(kc == KC - 1)),
                        rd=[r_wfm[s], r_xn], wr=[r_pp[b]])
                i4 = ch % 4
                if ch < CH_Q:
                    ph.op("act", lambda e, b=b, i4=i4: e.activation(out=stg_u[:, i4, :], in_=pp[b][:], func=AF.Copy),
                          rd=[r_pp[b]], wr=[r_su])
                    if i4 == 3:
                        ph.dma("sp", UT[:, t0:t0 + T].rearrange("(c p) t -> p c t", p=128), stg_u[:], key="su", rd=[r_su])
                elif ch < CH_C:
                    if (ch >= CH_OG and i4 == 0) or ch == CH_Q:
                        sfi = nsf % 2
                        nsf += 1
                    if ch < CH_OG:
                        slot = ch - CH_Q
                        func = AF.Copy
                    else:
                        slot = i4
                        func = AF.Silu if ch < CH_RIN else (AF.Copy if ch < CH_RG else AF.Gelu)
                    ph.op("act", lambda e, b=b, slot=slot, sfi=sfi, func=func: e.activation(
                        out=stg_f[sfi][:, slot, :], in_=pp[b][:], func=func), rd=[r_pp[b]], wr=[r_sf[sfi]])
                    if ch == CH_K + 1:
                        ph.dma("sp", QT[:, t0:t0 + T].rearrange("(c p) t -> p c t", p=128), stg_f[sfi][:, 0:2, :], key=f"sf{sfi}",
                               rd=[r_sf[sfi]])
                        ph.dma("sp", KT[:, t0:t0 + T].rearrange("(c p) t -> p c t", p=128), stg_f[sfi][:, 2:4, :], key=f"sf{sfi}",
                               rd=[r_sf[sfi]])
                    elif ch >= CH_OG and i4 == 3:
                        if ch < CH_RIN:
                            dst = SOG[:, t0:t0 + T]
                        elif ch < CH_RG:
                            dst = RIN[:, 3 + t0:3 + t0 + T]
                        else:
                            dst = GR[:, t0:t0 + T]
                        ph.dma("sp", dst.rearrange("(c p) t -> p c t", p=128), stg_f[sfi][:], key=f"sf{sfi}", rd=[r_sf[sfi]])
                elif ch < CH_A:
                    ci = (ch - CH_C) // 2
                    if (ch - CH_C) % 2 == 0:
                        if ci == 0:
                            sfi = nsf % 2
                            nsf += 1
                        ph.op("act", lambda e, b=b: e.activation(out=sig[:], in_=pp[b][:], func=AF.Sigmoid),
                              rd=[r_pp[b]], wr=[r_sig])
                    else:
                        ph.op("dve", lambda e, b=b, ci=ci, sfi=sfi: e.tensor_tensor(out=stg_f[sfi][:, ci, :], in0=pp[b][:], in1=sig[:],
                                                                                    op=ALU.mult),
                              rd=[r_pp[b], r_sig], wr=[r_sf[sfi]])
                        if ci == 3:
                            ph.dma("sp", UC[:, 15 + t0:15 + t0 + T].rearrange("(c p) t -> p c t", p=128), stg_f[sfi][:],
                                   key=f"sf{sfi}", rd=[r_sf[sfi]])
                else:
                    ph.op("act", lambda e, b=b: e.activation(out=stg_a[:], in_=pp[b][0:64, :], func=AF.Copy),
                          rd=[r_pp[b]], wr=[r_sa])
                    ph.dma("sp", AT[:, t0:t0 + T], stg_a[:], key="sa", rd=[r_sa])
            for sub in range(4):
                for kc in range(KC):
                    ph.op("pe", lambda e, sub=sub, kc=kc: e.matmul(
                        pk[:, 0:256], lhsT=xn[:, kc, sub * 128:(sub + 1) * 128], rhs=wtk[:, kc, 0:256], start=(kc == 0),
                        stop=(kc == KC - 1)), rd=[r_xn, r_wtk], wr=[r_pk])
                for kc in range(KC):
                    ph.op("pe", lambda e, sub=sub, kc=kc: e.matmul(
                        pv[:], lhsT=xn[:, kc, sub * 128:(sub + 1) * 128], rhs=wtk[:, kc, 256:768], start=(kc == 0),
                        stop=(kc == KC - 1)), rd=[r_xn, r_wtk], wr=[r_pv])
                ph.op("dve", lambda e, sub=sub: e.tensor_copy(out=stg_k[:, sub, :], in_=pk[:, 0:256]), rd=[r_pk], wr=[r_sk])
                ph.op("act", lambda e, sub=sub: e.activation(out=stg_v[:, sub, :], in_=pv[:], func=AF.Copy), rd=[r_pv], wr=[r_sv])
            ph.dma("sp", KTOK[t0:t0 + T, :].rearrange("(s p) n -> p s n", p=128), stg_k[:], key="sk", rd=[r_sk])
            ph.dma("sp", VTOK[t0:t0 + T, :].rearrange("(s p) n -> p s n", p=128), stg_v[:], key="sv", rd=[r_sv])
        ph.run()

        ph = Phase(nc, f"fo_{l}")
        uT = ph.sb("uT", [128, S], BF16)
        Z = ph.sb("Z", [128, 2, 128, N2], BF16)
        cs_f = ph.sb("cs_f", [128, 256], F32)
        cs = ph.sb("cs", [128, 256], BF16)
        sa_f = ph.sb("sa_f", [128, 2, 256], F32)
        sa = ph.sb("sa", [128, 2, 256], BF16)
        tw = ph.sb("tw", [128, 2, 128], F32)
        bd_f = ph.sb("bd_f", [128, 2, 128], F32)
        bd = ph.sb("bd", [128, 2, 128], BF16)
        NPB = 4
        apr = [ph.sb(f"apr{i}", [128, NPB, 128], BF16) for i in range(2)]
        api = [ph.sb(f"api{i}", [128, NPB, 128], BF16) for i in range(2)]
        t1 = ph.sb("t1", [128, 2, 128], F32)
        t2 = ph.sb("t2", [128, 2, 128], F32)
        yo = [ph.sb(f"yo{i}", [128, NPB, 128], BF16) for i in range(2)]
        pz = [ph.ps(f"pz{i}") for i in range(2)]
        pa = [ph.ps(f"pa{i}") for i in range(2)]
        pb = [ph.ps(f"pb{i}") for i in range(2)]
        r_c, r_u, r_Z = Res(), Res(), Res()
        r_pz, r_pa, r_pb = [Res(), Res()], [Res(), Res()], [Res(), Res()]
        r_ap, r_t1, r_t2, r_yo = [Res(), Res()], Res(), Res(), [Res(), Res()]
        ph.dma("sp", cs_f[:], c_cs128[:, :], key="c", wr=[r_c])
        ph.dma("sp", sa_f[:], c_sa.rearrange("a p n -> p a n"), key="c", wr=[r_c])
        ph.dma("sp", tw[:], c_tw.rearrange("a p n -> p a n"), key="c", wr=[r_c])
        ph.dma("sp", bd_f[:], c_bd.rearrange("a p n -> p a n"), key="c", wr=[r_c])
        ph.op("dve", lambda e: e.tensor_copy(out=cs[:], in_=cs_f[:]), rd=[r_c], wr=[r_c])
        ph.op("dve", lambda e: e.tensor_copy(out=sa[:], in_=sa_f[:]), rd=[r_c], wr=[r_c])
        ph.op("dve", lambda e: e.tensor_copy(out=bd[:], in_=bd_f[:]), rd=[r_c], wr=[r_c])
        nz = na = nb = 0
        for g in range(4):
            ph.dma("sp", uT[:], UT[g * 128:(g + 1) * 128, :], key="u", wr=[r_u])
            for s2 in range(0, N2, 2):
                b = nz % 2
                nz += 1
                for q in range(2):
                    ph.op("pe", lambda e, s2=s2, q=q, b=b: e.matmul(
                        pz[b][:, q * 256:(q + 1) * 256], lhsT=uT[:, s2 + q::N2], rhs=cs[:], start=True, stop=True),
                        rd=[r_u, r_c], wr=[r_pz[b]])
                for q in range(2):
                    ph.op("act", lambda e, s2=s2, b=b, q=q: e.activation(
                        out=Z[:, :, :, s2 + q], in_=pz[b][:, q * 256:(q + 1) * 256].rearrange("p (c n) -> p c n", c=2), func=AF.Copy),
                        rd=[r_pz[b]], wr=[r_Z])
            nsets = 128 // CHP
            for cs0 in range(0, nsets, NPB):
                ab = na % 2
                for q in range(NPB):
                    c0 = (cs0 + q) * CHP
                    b = na % 2
                    na += 1
                    zr_ap = Z[:, 0, c0:c0 + CHP, :].rearrange("p j s -> p (j s)")
                    zi_ap = Z[:, 1, c0:c0 + CHP, :].rearrange("p j s -> p (j s)")
                    ph.op("pe", lambda e, b=b, zr_ap=zr_ap: e.matmul(pa[b][:, 0:256], lhsT=zr_ap, rhs=sa[:, 0, :], start=True, stop=False),
                          rd=[r_Z, r_c], wr=[r_pa[b]])
                    ph.op("pe", lambda e, b=b, zi_ap=zi_ap: e.matmul(pa[b][:, 0:256], lhsT=zi_ap, rhs=sa[:, 1, :], start=False, stop=True),
                          rd=[r_Z, r_c], wr=[r_pa[b]])
                    ph.op("dve", lambda e, b=b: e.tensor_tensor(out=t1[:, 0, :], in0=pa[b][:, 0:128], in1=tw[:, 0, :], op=ALU.mult),
                          rd=[r_pa[b], r_c], wr=[r_t1])
                    ph.op("dve", lambda e, b=b: e.tensor_tensor(out=t1[:, 1, :], in0=pa[b][:, 128:256], in1=tw[:, 1, :], op=ALU.mult),
                          rd=[r_pa[b], r_c], wr=[r_t1])
                    ph.op("dve", lambda e, b=b: e.tensor_tensor(out=t2[:, 0, :], in0=pa[b][:, 0:128], in1=tw[:, 1, :], op=ALU.mult),
                          rd=[r_pa[b], r_c], wr=[r_t2])
                    ph.op("dve", lambda e, b=b: e.tensor_tensor(out=t2[:, 1, :], in0=pa[b][:, 128:256], in1=tw[:, 0, :], op=ALU.mult),
                          rd=[r_pa[b], r_c], wr=[r_t2])
                    sb_ = (cs0 // NPB) % 2
                    ph.op("pool", lambda e, q=q, sb_=sb_: e.tensor_tensor(out=apr[sb_][:, q, :], in0=t1[:, 0, :], in1=t1[:, 1, :],
                                                                          op=ALU.subtract), rd=[r_t1], wr=[r_ap[sb_]])
                    ph.op("pool", lambda e, q=q, sb_=sb_: e.tensor_tensor(out=api[sb_][:, q, :], in0=t2[:, 0, :], in1=t2[:, 1, :],
                                                                          op=ALU.add), rd=[r_t2], wr=[r_ap[sb_]])
                sb_ = (cs0 // NPB) % 2
                b = nb % 2
                nb += 1
                ph.op("pe", lambda e, b=b, sb_=sb_: e.matmul(pb[b][:], lhsT=bd[:, 0, :], rhs=apr[sb_][:].rearrange("p q n -> p (q n)"),
                                                            start=True, stop=False), rd=[r_ap[sb_], r_c], wr=[r_pb[b]])
                ph.op("pe", lambda e, b=b, sb_=sb_: e.matmul(pb[b][:], lhsT=bd[:, 1, :], rhs=api[sb_][:].rearrange("p q n -> p (q n)"),
                                                            start=False, stop=True), rd=[r_ap[sb_], r_c], wr=[r_pb[b]])
                ph.op("act", lambda e, b=b: e.activation(out=yo[b][:].rearrange("p q n -> p (q n)"), in_=pb[b][:], func=AF.Copy,
                                                         scale=1.0 / math.sqrt(S * 128.0)), rd=[r_pb[b]], wr=[r_yo[b]])
                row0 = g * 128 + cs0 * CHP
                dst = Y[row0:row0 + NPB * CHP, :].rearrange("(q j) (k2 k1) -> (j k2) q k1", q=NPB, k1=128)
                ph.dma("sp", dst, yo[b][:], key=f"yo{b}", rd=[r_yo[b]])
        ph.run()

        ph = Phase(nc, f"gl_{l}")
        SBK = 4
        wal_f = ph.sb("wal", [64, 2, 256], F32)
        bal_f = ph.sb("bal", [64, 2, 256], F32)
        gn = ph.sb("gn", [128, L], F32)
        ones = ph.sb("ones", [128, 128], F32)
        epst = ph.sb("eps", [128, 1], F32)
        ident = ph.sb("ident", [128, 128], F32)
        tri = ph.sb("tri", [128, 4, 128], F32)
        msk = ph.sb("msk", [128, 2, 128], F32)
        at_sb = ph.sb("at", [64, S], F32)
        qt_sb = [ph.sb(f"qt{i}", [128, 2, SBK * 128], F32) for i in range(2)]
        kt_sb = [ph.sb(f"kt{i}", [128, 2, SBK * 128], F32) for i in range(2)]
        ktok_sb = [ph.sb(f"ktok{i}", [128, SBK, 256], F32) for i in range(2)]
        vtok_sb = [ph.sb(f"vtok{i}", [128, SBK, 512], BF16) for i in range(2)]
        of_sb = [ph.sb(f"of{i}", [128, SBK, 512], F32) for i in range(2)]
        sog_sb = [ph.sb(f"sog{i}", [128, 4, SBK * 128], F32) for i in range(2)]
        yg_sb = [ph.sb(f"yg{i}", [128, 4, SBK * 128], BF16) for i in range(2)]
        ez = ph.sb("ez", [128, 256], F32)
        sp_ = ph.sb("sp", [128, 256], F32)
        btok = ph.sb("btok", [128, 256], F32)
        dtok = ph.sb("dtok", [128, 256], F32)
        khat = ph.sb("khat", [128, 256], BF16)
        E = ph.sb("E", [128, 2, 128], F32)
        Ei = ph.sb("Ei", [128, 2, 128], F32)
        qtl = ph.sb("qtl", [128, 2, 128], BF16)
        ktl = ph.sb("ktl", [128, 2, 128], BF16)
        scm = [ph.sb(f"scm{i}", [128, 128], BF16) for i in range(2)]
        st_f = ph.sb("st_f", [128, 2, 128], F32)
        st_b = ph.sb("st_b", [128, 4, 128], BF16)
        osum = ph.sb("osum", [128, 512], F32)
        sqo = ph.sb("sqo", [128, 512], F32)
        rsd = ph.sb("rsd", [128, 512], F32)
        yt = ph.sb("yt", [128, 512], F32)
        p_z = ph.ps("p_z")
        p_b = ph.ps("p_b")
        p_sc = [ph.ps(f"p_sc{i}") for i in range(2)]
        p_o = ph.ps("p_o")
        p_st = ph.ps("p_st")
        p_ot = ph.ps("p_ot")
        p_ss = ph.ps("p_ss")
        R = {n: Res() for n in ["c", "at", "ez", "sp", "btok", "dtok", "khat", "E", "Ei", "qtl", "ktl", "stf", "stb", "osum", "sqo",
                                "rsd", "yt", "pz", "pbf", "pb", "pdr", "po", "pst", "pot", "pss"]}
        r_scm, r_psc = [Res(), Res()], [Res(), Res()]
        r_q, r_k, r_kt, r_v, r_of, r_sog, r_yg = ([Res(), Res()] for _ in range(7))
        ph.dma("sp", wal_f[:], walpha[:, l * 512:(l + 1) * 512].rearrange("r (d n) -> r d n", d=2), key="c", wr=[R["c"]])
        ph.dma("sp", bal_f[:], balpha[:, l * 512:(l + 1) * 512].rearrange("r (d n) -> r d n", d=2), key="c", wr=[R["c"]])
        ph.dma("sp", gn[:], gnorm[:, :], key="c", wr=[R["c"]])
        ph.dma("sp", ident[:], c_ident[:, :], key="c", wr=[R["c"]])
        ph.dma("sp", tri[:], c_tri.rearrange("a p n -> p a n"), key="c", wr=[R["c"]])
        ph.dma("sp", msk[:], c_mask.rearrange("a p n -> p a n"), key="c", wr=[R["c"]])
        ph.dma("sp", at_sb[:], AT[:, :], key="at", wr=[R["at"]])
        ph.op("dve", lambda e: e.memset(ones[:], 1.0), wr=[R["c"]])
        ph.op("dve", lambda e: e.memset(epst[:], EPS), wr=[R["c"]])
        nsb = 0
        nsc = 0
        for d in range(2):
            ph.op("dve", lambda e: e.memset(st_f[:], 0.0), rd=[R["stf"]], wr=[R["stf"]])
            ph.op("dve", lambda e: e.memset(st_b[:], 0.0), rd=[R["stb"]], wr=[R["stb"]])
            blocks = list(range(NB)) if d == 0 else list(range(NB - 1, -1, -1))
            for bi, blk in enumerate(blocks):
                sbk = blk // SBK
                ib = blk % SBK
                if bi % SBK == 0:
                    sbuf_i = nsb % 2
                    nsb += 1
                    c0 = sbk * SBK * 128
                    c1 = c0 + SBK * 128
                    ph.dma("sp", qt_sb[sbuf_i][:], QT[:, c0:c1].rearrange("(c p) t -> p c t", p=128), key=f"q{sbuf_i}",
                           wr=[r_q[sbuf_i]])
                    ph.dma("sp", kt_sb[sbuf_i][:], KT[:, c0:c1].rearrange("(c p) t -> p c t", p=128), key=f"k{sbuf_i}",
                           wr=[r_k[sbuf_i]])
                    ph.dma("sp", ktok_sb[sbuf_i][:], KTOK[c0:c1, :].rearrange("(s p) n -> p s n", p=128), key=f"kt{sbuf_i}",
                           wr=[r_kt[sbuf_i]])
                    ph.dma("sp", vtok_sb[sbuf_i][:], VTOK[c0:c1, :].rearrange("(s p) n -> p s n", p=128), key=f"v{sbuf_i}",
                           wr=[r_v[sbuf_i]])
                    if d == 1:
                        ph.dma("sp", of_sb[sbuf_i][:], OF[c0:c1, :].rearrange("(s p) n -> p s n", p=128), key=f"of{sbuf_i}",
                               wr=[r_of[sbuf_i]])
                        ph.dma("sp", sog_sb[sbuf_i][:], SOG[:, c0:c1].rearrange("(c p) t -> p c t", p=128), key=f"sg{sbuf_i}",
                               wr=[r_sog[sbuf_i]])
                si = sbuf_i
                tk0 = blk * 128
                cl = slice(ib * 128, (ib + 1) * 128)
                ph.op("pe", lambda e, d=d, tk0=tk0: e.matmul(p_z[:, 0:256], lhsT=at_sb[32 * d:32 * d + 16, tk0:tk0 + 128],
                                                             rhs=wal_f[32 * d:32 * d + 16, d, :], start=True, stop=False),
                      rd=[R["at"], R["c"]], wr=[R["pz"]])
                ph.op("pe", lambda e, d=d: e.matmul(p_z[:, 0:256], lhsT=ones[32 * d:32 * d + 1, :], rhs=bal_f[32 * d:32 * d + 1, d, :], start=False, stop=True),
                      rd=[R["c"]], wr=[R["pz"]])
                ph.op("act", lambda e: e.activation(out=ez[:], in_=p_z[:, 0:256], func=AF.Exp, scale=-1.0), rd=[R["pz"]], wr=[R["ez"], R["pz"]])
                ph.op("act", lambda e: e.activation(out=sp_[:], in_=ez[:], func=AF.Ln, bias=1.0), rd=[R["ez"]], wr=[R["sp"]])
                ph.op("pe", lambda e, d=d: e.matmul(p_b[:, 0:256], lhsT=tri[:, 2 * d, :], rhs=sp_[:], start=True, stop=True),
                      rd=[R["sp"], R["c"]], wr=[R["pb"]])
                ph.op("pe", lambda e, d=d: e.matmul(p_b[:, 256:512], lhsT=tri[:, 2 * d + 1, :], rhs=sp_[:], start=True, stop=True),
                      rd=[R["sp"], R["c"]], wr=[R["pb"]])
                ph.op("dve", lambda e: e.tensor_copy(out=btok[:], in_=p_b[:, 0:256]), rd=[R["pb"]], wr=[R["btok"], R["pb"]])
                ph.op("act", lambda e: e.activation(out=dtok[:], in_=p_b[:, 256:512], func=AF.Exp), rd=[R["pb"]], wr=[R["dtok"], R["pb"]])
                ph.op("dve", lambda e, si=si, ib=ib: e.tensor_tensor(out=khat[:], in0=ktok_sb[si][:, ib, :], in1=dtok[:], op=ALU.mult),
                      rd=[R["dtok"], r_kt[si]], wr=[R["khat"]])
                for pr in range(2):
                    ph.op("pe", lambda e, pr=pr: e.transpose(out=p_z[:, 256 + pr * 128:256 + (pr + 1) * 128],
                                                             in_=btok[:, pr * 128:(pr + 1) * 128], identity=ident[:]),
                          rd=[R["btok"], R["c"]], wr=[R["pz"]])
                ph.op("act", lambda e: e.activation(out=E[:].rearrange("p a n -> p (a n)"), in_=p_z[:, 256:512], func=AF.Exp),
                      rd=[R["pz"]], wr=[R["E"], R["pz"]])
                ph.op("act", lambda e: e.activation(out=Ei[:].rearrange("p a n -> p (a n)"), in_=p_z[:, 256:512], func=AF.Exp, scale=-1.0),
                      rd=[R["pz"]], wr=[R["Ei"], R["pz"]])
                ph.op("dve", lambda e, si=si, cl=cl: e.scalar_tensor_tensor(out=qtl[:], in0=qt_sb[si][:, :, cl], scalar=0.125, in1=E[:],
                                                                            op0=ALU.mult, op1=ALU.mult),
                      rd=[R["E"], r_q[si]], wr=[R["qtl"]])
                ph.op("pool", lambda e, si=si, cl=cl: e.tensor_tensor(out=ktl[:], in0=kt_sb[si][:, :, cl], in1=Ei[:], op=ALU.mult),
                      rd=[R["Ei"], r_k[si]], wr=[R["ktl"]])
                for h in range(4):
                    pr, base = h // 2, (h % 2) * 64
                    b = nsc % 2
                    nsc += 1
                    ph.op("pe", lambda e, pr=pr, base=base, b=b: e.matmul(
                        p_sc[b][:, 0:128], lhsT=ktl[base:base + 64, pr, :], rhs=qtl[base:base + 64, pr, :], start=True, stop=True),
                        rd=[R["ktl"], R["qtl"]], wr=[r_psc[b]])
                    ph.op("dve", lambda e, b=b, d=d: e.tensor_tensor(out=scm[b][:], in0=p_sc[b][:, 0:128], in1=msk[:, d, :], op=ALU.mult),
                          rd=[r_psc[b], R["c"]], wr=[r_scm[b]])
                    ph.op("pe", lambda e, b=b, h=h, si=si, ib=ib: e.matmul(
                        p_o[:, h * 128:(h + 1) * 128], lhsT=scm[b][:], rhs=vtok_sb[si][:, ib, h * 128:(h + 1) * 128], start=True,
                        stop=False), rd=[r_scm[b], r_v[si]], wr=[R["po"]])
                    ph.op("pe", lambda e, h=h, pr=pr, base=base: e.matmul(
                        p_o[:, h * 128:(h + 1) * 128], lhsT=qtl[:, pr, :], rhs=st_b[:, h, :], start=False,
                        stop=True), rd=[R["qtl"], R["stb"]], wr=[R["po"]])
                for h in range(4):
                    pr, base = h // 2, (h % 2) * 64
                    ph.op("pe", lambda e, h=h, pr=pr, base=base, si=si, ib=ib: e.matmul(
                        p_st[base:base + 64, pr * 128:(pr + 1) * 128], lhsT=khat[:, h * 64:(h + 1) * 64],
                        rhs=vtok_sb[si][:, ib, h * 128:(h + 1) * 128], start=True, stop=True),
                        rd=[R["khat"], r_v[si]], wr=[R["pst"]])
                tl = 127 if d == 0 else 0
                for pr in range(2):
                    ph.op("dve", lambda e, pr=pr, tl=tl: e.scalar_tensor_tensor(
                        out=st_f[:, pr, :], in0=st_f[:, pr, :], scalar=E[:, pr, tl:tl + 1], in1=p_st[:, pr * 128:(pr + 1) * 128],
                        op0=ALU.mult, op1=ALU.add), rd=[R["pst"], R["E"], R["stf"]], wr=[R["stf"]])
                for hh in range(2):
                    ph.op("pool", lambda e, hh=hh: e.tensor_copy(out=st_b[hh * 64:(hh + 1) * 64, hh::2, :],
                                                                 in_=st_f[hh * 64:(hh + 1) * 64, :, :]), rd=[R["stf"]], wr=[R["stb"]])
                if d == 0:
                    ph.op("act", lambda e, si=si, ib=ib: e.activation(out=of_sb[si][:, ib, :], in_=p_o[:], func=AF.Copy),
                          rd=[R["po"]], wr=[r_of[si]])
                    if bi % SBK == SBK - 1:
                        c0 = sbk * SBK * 128
                        ph.dma("sp", OF[c0:c0 + SBK * 128, :].rearrange("(s p) n -> p s n", p=128), of_sb[si][:], key=f"o{si}",
                               rd=[r_of[si]])
                else:
                    ph.op("dve", lambda e, si=si, ib=ib: e.tensor_tensor(out=osum[:], in0=p_o[:], in1=of_sb[si][:, ib, :], op=ALU.add),
                          rd=[R["po"], r_of[si]], wr=[R["osum"]])
                    for h in range(4):
                        ph.op("pe", lambda e, h=h: e.transpose(out=p_ot[:, h * 128:(h + 1) * 128], in_=osum[:, h * 128:(h + 1) * 128],
                                                               identity=ident[:]), rd=[R["osum"], R["c"]], wr=[R["pot"]])
                    ph.op("act", lambda e: e.activation(out=sqo[:], in_=p_ot[:], func=AF.Square), rd=[R["pot"]], wr=[R["sqo"], R["pot"]])
                    ph.op("pe", lambda e: e.matmul(p_ss[:], lhsT=ones[:], rhs=sqo[:], start=True, stop=True),
                          rd=[R["sqo"], R["c"]], wr=[R["pss"]])
                    ph.op("act", lambda e: e.activation(out=rsd[:], in_=p_ss[:], func=AF.Sqrt, scale=1.0 / 128.0, bias=epst[:]),
                          rd=[R["pss"], R["c"]], wr=[R["rsd"]])
                    ph.op("dve", lambda e: e.reciprocal(out=rsd[:], in_=rsd[:]), rd=[R["rsd"]], wr=[R["rsd"]])
                    ph.op("dve", lambda e: e.scalar_tensor_tensor(out=yt[:], in0=p_ot[:], scalar=gn[:, l:l + 1], in1=rsd[:], op0=ALU.mult,
                                                                  op1=ALU.mult), rd=[R["pot"], R["rsd"], R["c"]], wr=[R["yt"], R["pot"]])
                    ph.op("pool", lambda e, si=si, cl=cl: e.tensor_tensor(
                        out=yg_sb[si][:, :, cl], in0=yt[:].rearrange("p (h t) -> p h t", h=4), in1=sog_sb[si][:, :, cl], op=ALU.mult),
                        rd=[R["yt"], r_sog[si]], wr=[r_yg[si]])
                    if bi % SBK == SBK - 1:
                        c0 = sbk * SBK * 128
                        ph.dma("sp", Y[512:1024, c0:c0 + SBK * 128].rearrange("(c p) t -> p c t", p=128), yg_sb[si][:], key=f"o{si}",
                               rd=[r_yg[si]])
        ph.run()

        ph = Phase(nc, f"lr_{l}")
        cw = ph.sb("cw", [128, 32], F32)
        cb = ph.sb("cb", [128, 8], F32)
        ba = ph.sb("ba", [128, 8], F32)
        bx = ph.sb("bx", [128, 8], F32)
        lam = ph.sb("lam", [128, 8], F32)
        cl1 = ph.sb("cl1", [128, 8], F32)
        cl2 = ph.sb("cl2", [128, 8], F32)
        zero1 = ph.sb("zero1", [128, 1], F32)
        wa_f = ph.sb("wa_f", [128, 8, 128], F32)
        wx_f = ph.sb("wx_f", [128, 8, 128], F32)
        wa_b = ph.sb("wa_b", [128, 8, 128], BF16)
        wx_b = ph.sb("wx_b", [128, 8, 128], BF16)
        xin = [ph.sb(f"xin{i}", [128, TT + 3], F32) for i in range(2)]
        xc = ph.sb("xc", [128, TT], F32)
        xcb = ph.sb("xcb", [128, TT], BF16)
        rg = ph.sb("rg", [128, TT], F32)
        ig = ph.sb("ig", [128, TT], F32)
        av = ph.sb("av", [128, TT], F32)
        a2 = ph.sb("a2", [128, TT], F32)
        uu = ph.sb("uu", [128, TT], F32)
        hh = [ph.sb(f"hh{i}", [128, TT], F32) for i in range(2)]
        hf = ph.sb("hf", [128, TT], F32)
        grt = ph.sb("grt", [128, TT], F32)
        yr = [ph.sb(f"yr{i}", [128, TT], BF16) for i in range(2)]
        carry = ph.sb("carry", [128, 1], F32)
        p_r = [ph.ps(f"p_r{i}") for i in range(2)]
        p_i = [ph.ps(f"p_i{i}") for i in range(2)]
        R = {n: Res() for n in ["c", "xc", "xcb", "rg", "ig", "av", "a2", "uu", "hf", "grt", "carry"]}
        r_xin, r_hh, r_yr, r_pr, r_pi = ([Res(), Res()] for _ in range(5))
        o8 = l * 8
        ph.op("dve", lambda e: e.memset(zero1[:], 0.0), wr=[R["c"]])
        ph.dma("sp", cw[:], lcw[:, l * 32:(l + 1) * 32], key="c", wr=[R["c"]])
        for tns, src in ((cb, lcb), (ba, lba), (bx, lbx), (lam, llam)):
            ph.dma("sp", tns[:], src[:, o8:o8 + 8], key="c", wr=[R["c"]])
        ph.dma("sp", wa_f[:], lwa[o8:o8 + 8].rearrange("a p n -> p a n"), key="c", wr=[R["c"]])
        ph.dma("sp", wx_f[:], lwx[o8:o8 + 8].rearrange("a p n -> p a n"), key="c", wr=[R["c"]])
        ph.op("dve", lambda e: e.tensor_copy(out=wa_b[:], in_=wa_f[:]), rd=[R["c"]], wr=[R["c"]])
        ph.op("dve", lambda e: e.tensor_copy(out=wx_b[:], in_=wx_f[:]), rd=[R["c"]], wr=[R["c"]])
        ph.op("act", lambda e: e.activation(out=lam[:], in_=lam[:], func=AF.Exp, scale=-1.0), rd=[R["c"]], wr=[R["c"]], strict=True)
        ph.op("dve", lambda e: e.tensor_scalar(out=cl1[:], in0=lam[:], scalar1=-0.25, scalar2=1.0 / 3.0, op0=ALU.mult, op1=ALU.add),
              rd=[R["c"]], wr=[R["c"]], strict=True)
        ph.op("dve", lambda e: e.tensor_tensor(out=cl1[:], in0=cl1[:], in1=lam[:], op=ALU.mult), rd=[R["c"]], wr=[R["c"]], strict=True)
        ph.op("dve", lambda e: e.tensor_scalar(out=cl1[:], in0=cl1[:], scalar1=1.0, scalar2=-0.5, op0=ALU.mult, op1=ALU.add), rd=[R["c"]], wr=[R["c"]], strict=True)
        ph.op("dve", lambda e: e.tensor_tensor(out=cl1[:], in0=cl1[:], in1=lam[:], op=ALU.mult), rd=[R["c"]], wr=[R["c"]], strict=True)
        ph.op("dve", lambda e: e.tensor_scalar(out=cl1[:], in0=cl1[:], scalar1=1.0, scalar2=1.0, op0=ALU.mult, op1=ALU.add), rd=[R["c"]], wr=[R["c"]], strict=True)
        ph.op("dve", lambda e: e.tensor_tensor(out=cl1[:], in0=cl1[:], in1=lam[:], op=ALU.mult), rd=[R["c"]], wr=[R["c"]], strict=True)
        ph.op("dve", lambda e: e.tensor_scalar(out=cl2[:], in0=cl1[:], scalar1=-16.0, scalar2=None, op0=ALU.mult), rd=[R["c"]], wr=[R["c"]], strict=True)
        ph.op("dve", lambda e: e.tensor_scalar(out=cl1[:], in0=cl1[:], scalar1=-8.0, scalar2=None, op0=ALU.mult), rd=[R["c"]], wr=[R["c"]], strict=True)
        nx = 0
        nh = 0
        npr = 0
        for d in range(2):
            for cc in range(4):
                ix = d * 4 + cc
                tiles = list(range(NTT)) if d == 0 else list(range(NTT - 1, -1, -1))
                for tix, tt in enumerate(tiles):
                    t0 = tt * TT
                    xi = nx % 2
                    nx += 1
                    lo = t0 if d == 0 else t0 + 3
                    ph.dma("sp", xin[xi][:], RIN[cc * 128:(cc + 1) * 128, lo:lo + TT + 3], key=f"x{xi}", wr=[r_xin[xi]])
                    for i in range(4):
                        sh = i if d == 0 else 3 - i
                        wsc = cw[:, ix * 4 + i:ix * 4 + i + 1]
                        if i == 0:
                            ph.op("dve", lambda e, xi=xi, sh=sh, wsc=wsc, ix=ix: e.tensor_scalar(
                                out=xc[:], in0=xin[xi][:, sh:sh + TT], scalar1=wsc, scalar2=cb[:, ix:ix + 1], op0=ALU.mult, op1=ALU.add),
                                rd=[r_xin[xi], R["c"]], wr=[R["xc"]])
                        else:
                            ph.op("dve", lambda e, xi=xi, sh=sh, wsc=wsc: e.scalar_tensor_tensor(
                                out=xc[:], in0=xin[xi][:, sh:sh + TT], scalar=wsc, in1=xc[:], op0=ALU.mult, op1=ALU.add),
                                rd=[r_xin[xi], R["c"], R["xc"]], wr=[R["xc"]])
                    ph.op("pool", lambda e: e.tensor_copy(out=xcb[:], in_=xc[:]), rd=[R["xc"]], wr=[R["xcb"]])
                    for sub in range(TT // 512):
                        b = npr % 2
                        npr += 1
                        sl = slice(sub * 512, (sub + 1) * 512)
                        ph.op("pe", lambda e, b=b, ix=ix, sl=sl: e.matmul(p_r[b][:], lhsT=wa_b[:, ix, :], rhs=xcb[:, sl], start=True, stop=True),
                              rd=[R["xcb"], R["c"]], wr=[r_pr[b]])
                        ph.op("pe", lambda e, b=b, ix=ix, sl=sl: e.matmul(p_i[b][:], lhsT=wx_b[:, ix, :], rhs=xcb[:, sl], start=True, stop=True),
                              rd=[R["xcb"], R["c"]], wr=[r_pi[b]])
                        ph.op("act", lambda e, b=b, ix=ix, sl=sl: e.activation(out=rg[:, sl], in_=p_r[b][:], func=AF.Sigmoid,
                                                                               bias=ba[:, ix:ix + 1], scale=1.0), rd=[r_pr[b], R["c"]], wr=[R["rg"]])
                        ph.op("act", lambda e, b=b, ix=ix, sl=sl: e.activation(out=ig[:, sl], in_=p_i[b][:], func=AF.Sigmoid,
                                                                               bias=bx[:, ix:ix + 1], scale=1.0), rd=[r_pi[b], R["c"]], wr=[R["ig"]])
                    def series(dst, clt, ix=ix):
                        ph.op("dve", lambda e: e.tensor_scalar(out=dst[:], in0=rg[:], scalar1=clt[:, ix:ix + 1], scalar2=None, op0=ALU.mult),
                              rd=[R["rg"], R["c"]], wr=[R["av"], R["a2"]])
                        ph.op("dve", lambda e: e.tensor_scalar(out=uu[:], in0=dst[:], scalar1=0.2, scalar2=1.0, op0=ALU.mult, op1=ALU.add),
                              rd=[R["av"], R["a2"]], wr=[R["uu"]])
                        for cf in (0.25, 1.0 / 3.0, 0.5):
                            ph.op("dve", lambda e: e.tensor_tensor(out=uu[:], in0=uu[:], in1=dst[:], op=ALU.mult), rd=[R["uu"]], wr=[R["uu"]])
                            ph.op("dve", lambda e, cf=cf: e.tensor_scalar(out=uu[:], in0=uu[:], scalar1=cf, scalar2=1.0, op0=ALU.mult, op1=ALU.add),
                                  rd=[R["uu"]], wr=[R["uu"]])
                    series(av, cl1)
                    ph.op("dve", lambda e: e.tensor_tensor(out=a2[:], in0=av[:], in1=uu[:], op=ALU.mult), rd=[R["uu"], R["av"]], wr=[R["a2"]])
                    ph.op("dve", lambda e: e.tensor_scalar(out=av[:], in0=a2[:], scalar1=1.0, scalar2=1.0, op0=ALU.mult, op1=ALU.add),
                          rd=[R["a2"]], wr=[R["av"]])
                    ph.op("dve", lambda e: e.tensor_scalar(out=uu[:], in0=av[:], scalar1=1.0, scalar2=1.0, op0=ALU.mult, op1=ALU.add),
                          rd=[R["av"]], wr=[R["uu"]])
                    ph.op("dve", lambda e: e.scalar_tensor_tensor(out=a2[:], in0=a2[:], scalar=-1.0, in1=uu[:], op0=ALU.mult, op1=ALU.mult),
                          rd=[R["uu"], R["a2"]], wr=[R["a2"]])
                    ph.op("act", lambda e: e.activation(out=a2[:], in_=a2[:], func=AF.Sqrt), rd=[R["a2"]], wr=[R["a2"]])
                    ph.op("pool", lambda e: e.tensor_tensor(out=uu[:], in0=ig[:], in1=xc[:], op=ALU.mult), rd=[R["ig"], R["xc"]], wr=[R["uu"]])
                    ph.op("pool", lambda e: e.tensor_tensor(out=uu[:], in0=uu[:], in1=a2[:], op=ALU.mult), rd=[R["uu"], R["a2"]], wr=[R["uu"]])
                    if DBGT is not None and d == 0 and cc == 0 and tix == 0 and l == 0:
                        for di, (tn, rn) in enumerate(((xc, "xc"), (rg, "rg"), (ig, "ig"), (av, "av"), (a2, "a2"), (uu, "uu"))):
                            ph.dma("sp", DBGT[di], tn[:], key="dbg", rd=[R[rn]])
                        ph.dma("sp", DBGC[:, 0:8], cl1[:], key="dbg", rd=[R["c"]])
                        ph.dma("sp", DBGC[:, 8:16], cl2[:], key="dbg", rd=[R["c"]])
                        ph.dma("sp", DBGC[:, 16:24], lam[:], key="dbg", rd=[R["c"]])
                        ph.dma("sp", DBGC[:, 24:32], cb[:], key="dbg", rd=[R["c"]])
                        ph.dma("sp", DBGC[:, 32:40], ba[:], key="dbg", rd=[R["c"]])
                        ph.dma("sp", DBGC[:, 40:48], bx[:], key="dbg", rd=[R["c"]])
                    hi = nh % 2
                    nh += 1
                    init = 0.0 if tix == 0 else carry[:, 0:1]
                    if d == 0:
                        ph.op("dve", lambda e, hi=hi, init=init: e.tensor_tensor_scan(out=hh[hi][:], data0=av[:], data1=uu[:], initial=init,
                                                                                      op0=ALU.mult, op1=ALU.add),
                              rd=[R["av"], R["uu"], R["carry"]], wr=[r_hh[hi]], strict=True)
                        ph.op("dve", lambda e, hi=hi: e.tensor_copy(out=carry[:], in_=hh[hi][:, TT - 1:TT]), rd=[r_hh[hi]], wr=[R["carry"]], strict=True)
                        ph.dma("sp", HF[cc * 128:(cc + 1) * 128, t0:t0 + TT], hh[hi][:], key=f"h{hi}", rd=[r_hh[hi]])
                    else:
                        ph.dma("sp", hf[:], HF[cc * 128:(cc + 1) * 128, t0:t0 + TT], key="hf", wr=[R["hf"]])
                        ph.dma("sp", grt[:], GR[cc * 128:(cc + 1) * 128, t0:t0 + TT], key="gr", wr=[R["grt"]])
                        ph.op("dve", lambda e, hi=hi, init=init: e.tensor_tensor_scan(out=hh[hi][:, ::-1], data0=av[:, ::-1], data1=uu[:, ::-1],
                                                                                      initial=init, op0=ALU.mult, op1=ALU.add),
                              rd=[R["av"], R["uu"], R["carry"]], wr=[r_hh[hi]], strict=True)
                        ph.op("dve", lambda e, hi=hi: e.tensor_copy(out=carry[:], in_=hh[hi][:, 0:1]), rd=[r_hh[hi]], wr=[R["carry"]], strict=True)
                        ph.op("pool", lambda e, hi=hi: e.tensor_tensor(out=hf[:], in0=hf[:], in1=hh[hi][:], op=ALU.add),
                              rd=[r_hh[hi], R["hf"]], wr=[R["hf"]])
                        ph.op("pool", lambda e, hi=hi: e.tensor_tensor(out=yr[hi][:], in0=hf[:], in1=grt[:], op=ALU.mult),
                              rd=[R["hf"], R["grt"]], wr=[r_yr[hi]])
                        ph.dma("sp", Y[1024 + cc * 128:1024 + (cc + 1) * 128, t0:t0 + TT], yr[hi][:], key=f"h{hi}", rd=[r_yr[hi]])
        ph.run()

        ph = Phase(nc, f"cv_{l}")
        w31 = ph.sb("w31", [128, 4 * 31], F32)
        b31 = ph.sb("b31", [128, 4], F32)
        lg = ph.sb("lg", [128, 4], F32)
        lb = ph.sb("lb", [128, 4], F32)
        ones = ph.sb("ones", [128, 128], F32)
        epst = ph.sb("eps", [128, 1], F32)
        uin = [ph.sb(f"uin{i}", [128, TT + 30], F32) for i in range(2)]
        ub = [ph.sb(f"ub{i}", [128, TT + 30], BF16) for i in range(2)]
        dg_f = ph.sb("dg_f", [128, 31, 128], F32)
        dg = ph.sb("dg", [128, 4, 31, 128], BF16)
        acc = ph.sb("acc", [128, 4, TT], F32)
        sqc = [ph.sb(f"sqc{i}", [128, 512], F32) for i in range(2)]
        mean = ph.sb("mean", [128, 512], F32)
        var = ph.sb("var", [128, 512], F32)
        tmpc = [ph.sb(f"tmpc{i}", [128, 512], F32) for i in range(2)]
        yc = [ph.sb(f"yc{i}", [128, 4, 512], BF16) for i in range(2)]
        p_s = ph.ps("p_s")
        p_q = ph.ps("p_q")
        pc = [ph.ps(f"pc{i}") for i in range(2)]
        R = {n: Res() for n in ["c", "mean", "var", "ps", "pq", "dgf", "dg"]}
        r_uin, r_ub, r_sqc, r_tmpc, r_yc, r_pc = ([Res(), Res()] for _ in range(6))
        r_acc = [[Res(), Res()] for _ in range(4)]
        for tns, src in ((b31, cfb), (lg, clg), (lb, clb)):
            ph.dma("sp", tns[:], src[:, l * 4:(l + 1) * 4], key="c", wr=[R["c"]])
        for cc in range(4):
            ph.dma("sp", dg_f[:], cfd[l * 4 + cc], key="dgf", wr=[R["dgf"]])
            ph.op("dve", lambda e, cc=cc: e.tensor_copy(out=dg[:, cc, :, :], in_=dg_f[:]), rd=[R["dgf"]], wr=[R["dg"]])
        ph.op("dve", lambda e: e.memset(ones[:], 1.0), wr=[R["c"]])
        ph.op("dve", lambda e: e.memset(epst[:], EPS), wr=[R["c"]])
        nu = 0
        nq = 0
        ny = 0
        npc = 0
        H2 = TT // 2
        for tt in range(NTT):
            t0 = tt * TT
            for cc in range(4):
                ui = nu % 2
                nu += 1
                ph.dma("sp", uin[ui][:], UC[cc * 128:(cc + 1) * 128, t0:t0 + TT + 30], key=f"u{ui}", wr=[r_uin[ui]])
                ph.op("pool", lambda e, ui=ui: e.tensor_copy(out=ub[ui][:], in_=uin[ui][:]), rd=[r_uin[ui]], wr=[r_ub[ui]])
                for sub in range(TT // 512):
                    b = npc % 2
                    npc += 1
                    half = (sub * 512) // H2
                    for i in range(31):
                        ph.op("pe", lambda e, b=b, cc=cc, i=i, ui=ui, sub=sub: e.matmul(
                            pc[b][:], lhsT=dg[:, cc, i, :], rhs=ub[ui][:, sub * 512 + i:sub * 512 + i + 512], start=(i == 0), stop=(i == 30)),
                            rd=[R["dg"], r_ub[ui]], wr=[r_pc[b]])
                    ph.op("act", lambda e, b=b, cc=cc, sub=sub: e.activation(
                        out=acc[:, cc, sub * 512:(sub + 1) * 512], in_=pc[b][:], func=AF.Identity, bias=b31[:, cc:cc + 1], scale=1.0),
                        rd=[r_pc[b], R["c"]], wr=[r_acc[cc][half]])
            for sub in range(TT // 512):
                sl = slice(sub * 512, (sub + 1) * 512)
                half = (sub * 512) // H2
                for cc in range(4):
                    ph.op("pe", lambda e, cc=cc, sl=sl: e.matmul(p_s[:], lhsT=ones[:], rhs=acc[:, cc, sl], start=(cc == 0), stop=(cc == 3)),
                          rd=[r_acc[cc][half], R["c"]], wr=[R["ps"]])
                for cc in range(4):
                    qi = nq % 2
                    nq += 1
                    ph.op("act", lambda e, cc=cc, sl=sl, qi=qi: e.activation(out=sqc[qi][:], in_=acc[:, cc, sl], func=AF.Square),
                          rd=[r_acc[cc][half]], wr=[r_sqc[qi]])
                    ph.op("pe", lambda e, cc=cc, qi=qi: e.matmul(p_q[:], lhsT=ones[:], rhs=sqc[qi][:], start=(cc == 0), stop=(cc == 3)),
                          rd=[r_sqc[qi], R["c"]], wr=[R["pq"]])
                ph.op("dve", lambda e: e.tensor_scalar(out=mean[:], in0=p_s[:], scalar1=1.0 / 512.0, scalar2=None, op0=ALU.mult),
                      rd=[R["ps"]], wr=[R["mean"]])
                ph.op("dve", lambda e: e.tensor_tensor(out=var[:], in0=mean[:], in1=mean[:], op=ALU.mult), rd=[R["mean"]], wr=[R["var"]])
                ph.op("dve", lambda e: e.scalar_tensor_tensor(out=var[:], in0=p_q[:], scalar=1.0 / 512.0, in1=var[:], op0=ALU.mult,
                                                              op1=ALU.subtract), rd=[R["pq"], R["var"]], wr=[R["var"]])
                ph.op("act", lambda e: e.activation(out=var[:], in_=var[:], func=AF.Sqrt, bias=epst[:]), rd=[R["var"], R["c"]], wr=[R["var"]])
                ph.op("dve", lambda e: e.reciprocal(out=var[:], in_=var[:]), rd=[R["var"]], wr=[R["var"]])
                yi = ny % 2
                ny += 1
                for cc in range(4):
                    b = cc % 2
                    ph.op("dve", lambda e, cc=cc, sl=sl, b=b: e.tensor_tensor(out=tmpc[b][:], in0=acc[:, cc, sl], in1=mean[:], op=ALU.subtract),
                          rd=[r_acc[cc][half], R["mean"]], wr=[r_tmpc[b]])
                    ph.op("pool", lambda e, b=b: e.tensor_tensor(out=tmpc[b][:], in0=tmpc[b][:], in1=var[:], op=ALU.mult),
                          rd=[r_tmpc[b], R["var"]], wr=[r_tmpc[b]])
                    ph.op("act", lambda e, cc=cc, b=b, yi=yi: e.activation(out=yc[yi][:, cc, :], in_=tmpc[b][:], func=AF.Silu,
                                                                           scale=lg[:, cc:cc + 1], bias=lb[:, cc:cc + 1]),
                          rd=[r_tmpc[b], R["c"]], wr=[r_yc[yi]])
                c0 = t0 + sub * 512
                ph.dma("sp", Y[1536:2048, c0:c0 + 512].rearrange("(c p) t -> p c t", p=128), yc[yi][:], key=f"y{yi}", rd=[r_yc[yi]])
        ph.run()

        ph = Phase(nc, f"f2_{l}")
        tc = TileCtx(ph, with_xT=False)
        ffn = FFN(ph, tc)
        ytile = ffn.hT
        cnt = [0]
        x_dst = xB if l < L - 1 else yT_out
        r_xC = Res()
        for ti in range(NT):
            t0 = ti * T
            ph.dma("sp", ytile[:, 0:KC, :], Y[:, t0:t0 + T].rearrange("(c p) t -> p c t", p=128), key="yt",
                   wr=[ffn.r_h[i] for i in range(KC)])
            for dc in range(KC):
                s = ffn.nout % 2
                ffn.nout += 1
                ph.dma("sp", ffn.wout[s][:, 0:KC, :], wmo_b[l, dc], key=f"wout{s}", wr=[ffn.r_wout[s]])
                for kc in range(KC):
                    ph.op("pe", lambda e, s=s, kc=kc: e.matmul(ffn.po[s][:], lhsT=ffn.wout[s][:, kc, :], rhs=ytile[:, kc, :],
                                                               start=(kc == 0), stop=(kc == KC - 1)),
                          rd=[ffn.r_wout[s], ffn.r_h[kc]], wr=[ffn.r_po[s]])
                ph.op("act", lambda e, s=s, dc=dc: e.activation(out=tc.xo[:, dc, :], in_=ffn.po[s][:], func=AF.Copy),
                      rd=[ffn.r_po[s]], wr=[tc.r_xo])
            post_stream(ph, tc, l, 1, cnt, xA, t0)
            store_xo(ph, tc, xC, t0, dres=r_xC)
            pre_norm(ph, tc, ffn.xn, ffn.r_xn, l, 2, cnt, src=tc.xo, src_res=[tc.r_xo] * KC)
            ffn.tile_in(l, 1)
            ffn.tile_out(l, 1)
            post_stream(ph, tc, l, 2, cnt, xC, t0, dres=r_xC)
            store_xo(ph, tc, x_dst, t0)
        ph.run()


def _consts(S):
    N2 = S // 128
    CHP = 128 // N2
    c = {}
    c["c_ident"] = np.eye(128, dtype=np.float32)
    s = np.arange(128)[:, None]
    t = np.arange(128)[None, :]
    v = np.float32(-1.0 / 16.0)
    trif = np.where(s <= t, v, 0).astype(np.float32)
    remf = np.where(s > t, v, 0).astype(np.float32)
    trib = np.where(s >= t, v, 0).astype(np.float32)
    remb = np.where(s < t, v, 0).astype(np.float32)
    c["c_tri"] = np.stack([trif, remf, trib, remb])
    c["c_mask"] = np.stack([(s <= t), (s >= t)]).astype(np.float32)
    ang = 2 * np.pi * (s * t % 128) / 128.0
    C, Sn = np.cos(ang), np.sin(ang)
    c["c_cs128"] = np.concatenate([C, -Sn], 1).astype(np.float32)
    c["c_sa"] = np.stack([np.concatenate([C, -Sn], 1), np.concatenate([Sn, C], 1)]).astype(np.float32)
    m = np.arange(128)
    s2 = (m % N2)[:, None]
    k1 = np.arange(128)[None, :]
    th = 2 * np.pi * (s2 * k1 % S) / S
    c["c_tw"] = np.stack([np.cos(th), -np.sin(th)]).astype(np.float32)
    j_r = (m // N2)[:, None]
    j_c = (m // N2)[None, :]
    a2 = (m % N2)[:, None]
    b2 = (m % N2)[None, :]
    th2 = 2 * np.pi * (a2 * b2 % N2) / N2
    same = (j_r == j_c)
    c["c_bd"] = np.stack([np.where(same, np.cos(th2), 0), np.where(same, np.sin(th2), 0)]).astype(np.float32)
    return c


def _pp(v, n):
    v = np.asarray(v, np.float32)
    lead = v.shape[:-1]
    return np.ascontiguousarray(np.moveaxis(v.reshape(lead + (n, 128)), -1, 0)).reshape(128, -1)


def prep_shared(inp, S):
    f = lambda a: np.ascontiguousarray(np.asarray(a, dtype=np.float32))
    g = {}
    w_ada = f(inp["w_ada"])
    g["wada"] = np.ascontiguousarray(w_ada.reshape(L, KC, 128, 36, 512).transpose(0, 3, 2, 1, 4))
    g["bada"] = _pp(f(inp["b_ada"]), 144)
    g["gpre"] = _pp(f(inp["g_pre"]), 16)
    g["gpost"] = _pp(f(inp["g_post"]), 16)
    for nm, key in (("w1", "ffn1"), ("w2", "ffn2")):
        w_in = f(inp[key + "_w_in"])
        DFF = w_in.shape[2] // 2
        HCn = DFF // 128
        up = w_in[:, :, :DFF].reshape(L, KC, 128, HCn, 128)
        gt = w_in[:, :, DFF:].reshape(L, KC, 128, HCn, 128)
        g[nm + "in"] = np.ascontiguousarray(np.concatenate([up, gt], -1).transpose(0, 3, 2, 1, 4))
        w_out = f(inp[key + "_w_out"])
        HC = DFF // 128
        g[nm + "out"] = np.ascontiguousarray(w_out.reshape(L, HC, 128, KC, 128).transpose(0, 3, 2, 1, 4))
    wm = f(inp["w_mix_in"])
    o = np.cumsum([0, 512, 256, 256, 512, 512, 32, 512, 512, 1024])
    fcol, qcol, kcol, vcol, ogcol, acol, rincol, rgcol, ccol = [np.arange(o[i], o[i + 1]) for i in range(9)]
    cv, cg = ccol[:512], ccol[512:]
    cinter = np.concatenate([np.concatenate([cg[i * 128:(i + 1) * 128], cv[i * 128:(i + 1) * 128]]) for i in range(4)])
    order = np.concatenate([fcol, qcol, kcol, ogcol, rincol, rgcol, cinter])
    wfm = np.zeros((L, D, NCH_FM * 128), np.float32)
    wfm[:, :, :28 * 128] = wm[:, :, order]
    wfm[:, :, 28 * 128:28 * 128 + 16] = wm[:, :, acol[:16]]
    wfm[:, :, 28 * 128 + 32:28 * 128 + 48] = wm[:, :, acol[16:]]
    g["wmi"] = np.ascontiguousarray(wfm.reshape(L, KC, 128, NCH_FM, 128).transpose(0, 3, 2, 1, 4))
    g["wmt"] = np.ascontiguousarray(wm[:, :, np.concatenate([kcol, vcol])].reshape(L, KC, 128, 768).transpose(0, 2, 1, 3))
    g["wmo"] = np.ascontiguousarray(f(inp["w_mix_out"]).reshape(L, KC, 128, KC, 128).transpose(0, 3, 2, 1, 4))
    wal = f(inp["gla_w_alpha"])
    wal64 = np.zeros((64, L * 2 * 256), np.float32)
    for l in range(L):
        for d in range(2):
            wal64[32 * d:32 * d + 16, (l * 2 + d) * 256:(l * 2 + d + 1) * 256] = wal[l, d]
    g["walpha"] = wal64
    bal = f(inp["gla_b_alpha"])
    bal64 = np.zeros((64, L * 2 * 256), np.float32)
    for l in range(L):
        for d in range(2):
            bal64[32 * d, (l * 2 + d) * 256:(l * 2 + d + 1) * 256] = bal[l, d]
    g["balpha"] = bal64
    g["gnorm"] = np.ascontiguousarray(f(inp["gla_norm_g"]).T)
    lcw = f(inp["lru_conv_w"])
    g["lcw"] = np.ascontiguousarray(lcw.reshape(L, 2, 4, 4, 128).transpose(4, 0, 1, 3, 2)).reshape(128, -1)
    for nm, key in (("lcb", "lru_conv_b"), ("lba", "lru_b_a"), ("lbx", "lru_b_x"), ("llam", "lru_lambda")):
        g[nm] = _pp(f(inp[key]), 4)
    for nm, key in (("lwa", "lru_w_a"), ("lwx", "lru_w_x")):
        w = f(inp[key])
        m = np.zeros((L, 2, 4, 128, 128), np.float32)
        for cc in range(4):
            m[:, :, cc, 0:64, 0:64] = w[:, :, 2 * cc]
            m[:, :, cc, 64:128, 64:128] = w[:, :, 2 * cc + 1]
        g[nm] = m.reshape(L * 2 * 4, 128, 128)
    cw = f(inp["conf_dw_w"])
    g["cfw"] = np.ascontiguousarray(cw.reshape(L, 31, 4, 128).transpose(3, 0, 2, 1)).reshape(128, -1)
    cfd = np.zeros((L, 4, 128, 31, 128), np.float32)
    idx = np.arange(128)
    cfd[:, :, idx, :, idx] = cw.reshape(L, 31, 4, 128).transpose(3, 0, 2, 1)
    g["cfd"] = cfd.reshape(L * 4, 128, 31, 128)
    g["cfb"] = _pp(f(inp["conf_dw_b"]), 4)
    g["clg"] = _pp(f(inp["conf_ln_g"]), 4)
    g["clb"] = _pp(f(inp["conf_ln_b"]), 4)
    g.update(_consts(S))
    return g


_NC_CACHE = {}


def run_seqs(xs, cs, inp):
    S = xs[0].shape[0]
    DFF = inp["ffn1_w_in"].shape[2] // 2
    key = (S, DFF)
    if key not in _NC_CACHE:
        _NC_CACHE[key] = build_program(S, DFF)
    nc = _NC_CACHE[key]
    shared = prep_shared(inp, S)
    n = DBG.get("ncores", 8)
    in_maps = []
    for i in range(n):
        j = i if i < len(xs) else 0
        m = dict(shared)
        m["xT"] = np.ascontiguousarray(np.asarray(xs[j], np.float32).T)
        m["cT"] = np.ascontiguousarray(np.asarray(cs[j], np.float32).reshape(KC, 128).T)
        in_maps.append(m)
    res = run_bass_kernel_spmd(nc, in_maps, core_ids=list(range(n)))
    if DBG["outs"]:
        DBG["res"] = res.results
    return [np.ascontiguousarray(res.results[i]["yT"].T) for i in range(len(xs))]


def kernel(**inp):
    xp = np.asarray(inp["x_prompt"], np.float32)
    xs_ = np.asarray(inp["x_sample"], np.float32)
    cp = np.asarray(inp["c_prompt"], np.float32)
    cs_ = np.asarray(inp["c_sample"], np.float32)
    xs = [xp[b] for b in range(xp.shape[0])] + [xs_[b] for b in range(xs_.shape[0])]
    cs = [cp[b] for b in range(cp.shape[0])] + [cs_[b] for b in range(cs_.shape[0])]
    outs = run_seqs(xs, cs, inp)
    nb = xp.shape[0]
    y_p = np.stack(outs[:nb]).astype(np.float32)
    y_s = np.stack(outs[nb:]).astype(np.float32)
    return (y_p, y_s)
```

```python
import math
from contextlib import ExitStack
import numpy as np
import concourse.bass as bass
import concourse.mybir as mybir
from concourse.bass_utils import run_bass_kernel_spmd

F32 = mybir.dt.float32
BF16 = mybir.dt.bfloat16
AF = mybir.ActivationFunctionType
ALU = mybir.AluOpType

D = 2048
KC = 16
L = 2
EPS = 1e-6
T = 512
NCH_FM = 29

CH_U, CH_Q, CH_K, CH_OG, CH_RIN, CH_RG, CH_C, CH_A = 0, 4, 6, 8, 12, 16, 20, 28


class Op:
    __slots__ = ("eng", "fn", "deps", "is_dma", "key", "dval", "signal", "ticket")

    def __init__(self, eng, fn, is_dma=False):
        self.eng = eng
        self.fn = fn
        self.deps = []
        self.is_dma = is_dma
        self.key = None
        self.dval = 0
        self.signal = False
        self.ticket = 0


class Res:
    __slots__ = ("w", "r")

    def __init__(self):
        self.w = None
        self.r = {}


ENGS = ("pe", "act", "dve", "pool", "sp")
GLOB = {}


class Phase:
    def __init__(self, nc, name):
        self.nc = nc
        self.name = name
        self.ops = {e: [] for e in ENGS}
        self.ctx = ExitStack()
        self.dma_cnt = {}
        self.n = 0

    def sb(self, name, shape, dt):
        return self.ctx.enter_context(self.nc.sbuf_tensor(f"{self.name}_{name}", shape, dt))

    def ps(self, name, shape=(128, 512), dt=F32):
        return self.ctx.enter_context(self.nc.psum_tensor(f"{self.name}_{name}", list(shape), dt))

    def _mk(self, eng, fn, rd, wr, is_dma=False, strict=False):
        op = Op(eng, fn, is_dma)
        deps = []
        for r in rd:
            if r.w is not None:
                deps.append(r.w)
        for w in wr:
            if w.w is not None:
                deps.append(w.w)
            deps.extend(w.r.values())
        seen = set()
        for d in deps:
            if id(d) in seen or d is op:
                continue
            seen.add(id(d))
            if d.is_dma or d.eng != eng or is_dma or strict:
                op.deps.append(d)
                if not d.is_dma:
                    d.signal = True
        for r in rd:
            if is_dma:
                r.r[("dma", self.n)] = op
            else:
                r.r[eng] = op
        for w in wr:
            w.w = op
            w.r = {}
        self.n += 1
        self.ops[eng].append(op)
        return op

    def op(self, eng, fn, rd=(), wr=(), strict=False):
        return self._mk(eng, fn, rd, wr, strict=strict)

    def dma(self, eng, out, in_, key, rd=(), wr=(), **kw):
        def fn(e):
            return e.dma_start(out=out, in_=in_, **kw)
        op = self._mk(eng, fn, rd, wr, is_dma=True)
        c = self.dma_cnt.get(key, 0) + 16
        self.dma_cnt[key] = c
        op.key = key
        op.dval = c
        return op

    def run(self):
        nc = self.nc
        if DBG.get("only") and self.name not in DBG["only"]:
            self.ctx.close()
            if DBG["stop"] == self.name:
                raise _Stop()
            return
        G = GLOB[id(nc)]
        engs = {"pe": "tensor", "act": "scalar", "dve": "vector", "pool": "gpsimd", "sp": "sync"}
        for e in ENGS:
            if e not in G["esem"]:
                G["esem"][e] = G["st"].enter_context(nc.semaphore(f"s_{e}"))
                G["ecnt"][e] = 0
        for k in self.dma_cnt:
            if k not in G["dsem"]:
                G["dsem"][k] = G["st"].enter_context(nc.semaphore(f"d_{k}"))
                G["dcnt"][k] = 0
        esem, dsem = G["esem"], G["dsem"]
        ebase = dict(G["ecnt"])
        dbase = dict(G["dcnt"])
        for e in ENGS:
            t = ebase[e]
            for op in self.ops[e]:
                if op.signal and not op.is_dma:
                    t += 1
                    op.ticket = t
            G["ecnt"][e] = t
        for k, c in self.dma_cnt.items():
            G["dcnt"][k] = dbase[k] + c
        with ExitStack() as st:
            block = st.enter_context(nc.Block())

            def emit(e, eng):
                waited = {}
                for op in self.ops[e]:
                    for d in op.deps:
                        if d.is_dma:
                            k, v, s = ("d", d.key), dbase[d.key] + d.dval, dsem[d.key]
                        else:
                            k, v, s = ("e", d.eng), d.ticket, esem[d.eng]
                        if waited.get(k, 0) >= v:
                            continue
                        waited[k] = v
                        eng.wait_ge(s, v)
                    ins = op.fn(eng)
                    if op.is_dma:
                        ins.then_inc(dsem[op.key], 16)
                    elif op.signal:
                        ins.then_inc(esem[e], 1)
                if e == "sp":
                    for k, c in self.dma_cnt.items():
                        if waited.get(("d", k), 0) < dbase[k] + c:
                            eng.wait_ge(dsem[k], dbase[k] + c)

            for e in ENGS:
                getattr(block, engs[e])(lambda eng, e=e: emit(e, eng))
        self.ctx.close()
        if DBG["stop"] == self.name:
            raise _Stop()


DBG = {"stop": None, "outs": ()}


class _Stop(Exception):
    pass


def build_program(S, DFF):
    nc = bass.Bass("TRN2", target_bir_lowering=False)
    GLOB[id(nc)] = {"st": ExitStack(), "esem": {}, "dsem": {}, "ecnt": {}, "dcnt": {}}
    try:
        _build(nc, S, DFF)
    except _Stop:
        pass
    GLOB[id(nc)]["st"].close()
    return nc


def _build(nc, S, DFF):
    NT = S // T
    HC = DFF // 128
    HG = HC // 2
    N2 = S // 128
    CHP = 128 // N2
    NB = S // 128
    TT = min(2048, S)
    NTT = S // TT
    def din(name, shape, dt=F32):
        return nc.dram_tensor(name, list(shape), dt, kind="ExternalInput").ap()

    def dscr(name, shape, dt=F32):
        kind = "ExternalOutput" if name in DBG["outs"] else "Internal"
        return nc.dram_tensor(name, list(shape), dt, kind=kind).ap()

    xT_in = din("xT", [D, S])
    cT_in = din("cT", [128, KC])
    wada = din("wada", [L, 36, 128, KC, 512])
    bada = din("bada", [128, L * 144])
    gpre = din("gpre", [128, L * 48])
    gpost = din("gpost", [128, L * 48])
    win_f = [din("w1in", [L, HC, 128, KC, 256]), din("w2in", [L, HC, 128, KC, 256])]
    wout_f = [din("w1out", [L, KC, 128, HC, 128]), din("w2out", [L, KC, 128, HC, 128])]
    wmi_f = din("wmi", [L, NCH_FM, 128, KC, 128])
    wmt_f = din("wmt", [L, 128, KC, 768])
    wmo_f = din("wmo", [L, KC, 128, KC, 128])
    walpha = din("walpha", [64, L * 2 * 256])
    balpha = din("balpha", [64, L * 2 * 256])
    gnorm = din("gnorm", [128, L])
    lcw = din("lcw", [128, L * 2 * 4 * 4])
    lcb = din("lcb", [128, L * 2 * 4])
    lwa = din("lwa", [L * 2 * 4, 128, 128])
    lwx = din("lwx", [L * 2 * 4, 128, 128])
    lba = din("lba", [128, L * 2 * 4])
    lbx = din("lbx", [128, L * 2 * 4])
    llam = din("llam", [128, L * 2 * 4])
    cfw = din("cfw", [128, L * 4 * 31])
    cfd = din("cfd", [L * 4, 128, 31, 128])
    cfb = din("cfb", [128, L * 4])
    clg = din("clg", [128, L * 4])
    clb = din("clb", [128, L * 4])
    c_ident = din("c_ident", [128, 128])
    c_tri = din("c_tri", [4, 128, 128])
    c_mask = din("c_mask", [2, 128, 128])
    c_cs128 = din("c_cs128", [128, 256])
    c_sa = din("c_sa", [2, 128, 256])
    c_tw = din("c_tw", [2, 128, 128])
    c_bd = din("c_bd", [2, 128, 128])
    yT_out = nc.dram_tensor("yT", [D, S], F32, kind="ExternalOutput").ap()

    win_b = [dscr("w1in_b", [L, HC, 128, KC, 256], BF16), dscr("w2in_b", [L, HC, 128, KC, 256], BF16)]
    wout_b = [dscr("w1out_b", [L, KC, 128, HC, 128], BF16), dscr("w2out_b", [L, KC, 128, HC, 128], BF16)]
    wmi_b = dscr("wmi_b", [L, NCH_FM, 128, KC, 128], BF16)
    wmt_b = dscr("wmt_b", [L, 128, KC, 768], BF16)
    wmo_b = dscr("wmo_b", [L, KC, 128, KC, 128], BF16)
    mods_d = dscr("mods", [128, L * 144])
    xA = dscr("xA", [D, S])
    xB = dscr("xB", [D, S])
    xC = dscr("xC", [D, S])
    UT = dscr("UT", [512, S], BF16)
    QT = dscr("QT", [256, S])
    KT = dscr("KT", [256, S])
    SOG = dscr("SOG", [512, S])
    RIN = dscr("RIN", [512, S + 6])
    GR = dscr("GR", [512, S])
    UC = dscr("UC", [512, S + 30])
    AT = dscr("AT", [64, S])
    KTOK = dscr("KTOK", [S, 256])
    VTOK = dscr("VTOK", [S, 512], BF16)
    OF = dscr("OF", [S, 512])
    HF = dscr("HF", [512, S])
    Y = dscr("Y", [D, S], BF16)
    DBGT = dscr("DBGT", [8, 128, TT]) if "DBGT" in DBG["outs"] else None
    DBGC = dscr("DBGC", [128, 48]) if "DBGT" in DBG["outs"] else None

    ph = Phase(nc, "cvt")
    k = 0
    r_cvt = [Res() for _ in range(4)]

    def cvt(ph, dst, src, rows_per=8):
        nonlocal k
        R = src.shape[0]
        for r0 in range(0, R, rows_per):
            r1 = min(R, r0 + rows_per)
            ph.dma("pool", dst[r0:r1], src[r0:r1], key=f"c{k % 4}", wr=[r_cvt[k % 4]], max_dma_last_dim=8192)
            k += 1

    def cvt_layer(ph, l):
        for i in range(2):
            cvt(ph, win_b[i][l].rearrange("g p k n -> (g p) (k n)"), win_f[i][l].rearrange("g p k n -> (g p) (k n)"), 256)
            cvt(ph, wout_b[i][l].rearrange("g p k n -> (g p) (k n)"), wout_f[i][l].rearrange("g p k n -> (g p) (k n)"), 256)
        cvt(ph, wmi_b[l].rearrange("g p k n -> (g p) (k n)"), wmi_f[l].rearrange("g p k n -> (g p) (k n)"), 512)
        cvt(ph, wmt_b[l].rearrange("p k n -> p (k n)"), wmt_f[l].rearrange("p k n -> p (k n)"), 128)
        cvt(ph, wmo_b[l].rearrange("g p k n -> (g p) (k n)"), wmo_f[l].rearrange("g p k n -> (g p) (k n)"), 512)

    for l in range(L):
        cvt_layer(ph, l)
    zt = ph.sb("zt", [128, 32], F32)
    zr = Res()
    ph.op("dve", lambda e: e.memset(zt[:], 0.0), wr=[zr])
    for cc in range(4):
        ph.dma("sp", RIN[cc * 128:(cc + 1) * 128, 0:3], zt[:, 0:3], key="z", rd=[zr])
        ph.dma("sp", RIN[cc * 128:(cc + 1) * 128, S + 3:S + 6], zt[:, 0:3], key="z", rd=[zr])
        ph.dma("sp", UC[cc * 128:(cc + 1) * 128, 0:15], zt[:, 0:15], key="z", rd=[zr])
        ph.dma("sp", UC[cc * 128:(cc + 1) * 128, S + 15:S + 30], zt[:, 0:15], key="z", rd=[zr])

    ct = ph.sb("ct", [128, KC], F32)
    sc = ph.sb("sc", [128, KC], F32)
    bad = ph.sb("bad", [128, L * 144], F32)
    gpr = ph.sb("gpr", [128, L * 48], F32)
    gpo = ph.sb("gpo", [128, L * 48], F32)
    modr = ph.sb("modr", [128, 144], F32)
    modo = ph.sb("modo", [128, L * 144], F32)
    wsl = [ph.sb(f"w{i}", [128, KC, 512], F32) for i in range(2)]
    pm = ph.ps("pm")
    r_ct, r_sc, r_small, r_pm, r_modr, r_modo = Res(), Res(), Res(), Res(), Res(), Res()
    r_w = [Res(), Res()]
    ph.dma("sp", ct[:], cT_in[:, :], key="ld0", wr=[r_ct])
    ph.dma("sp", bad[:], bada[:, :], key="ld1", wr=[r_small])
    ph.dma("sp", gpr[:], gpre[:, :], key="ld1", wr=[r_small])
    ph.dma("sp", gpo[:], gpost[:, :], key="ld1", wr=[r_small])
    ph.op("act", lambda e: e.activation(out=sc[:], in_=ct[:], func=AF.Silu), rd=[r_ct], wr=[r_sc])
    it = 0
    for l in range(L):
        for cg in range(36):
            s = it % 2
            it += 1
            ph.dma("sp", wsl[s][:], wada[l, cg], key=f"w{s}", wr=[r_w[s]])
            for sub in range(4):
                q = cg * 4 + sub
                for kc in range(KC):
                    ph.op("pe", lambda e, s=s, sub=sub, kc=kc, q=q: e.matmul(
                        pm[:, q:q + 1], lhsT=wsl[s][:, kc, sub * 128:(sub + 1) * 128], rhs=sc[:, kc:kc + 1],
                        start=(kc == 0), stop=(kc == KC - 1)), rd=[r_w[s], r_sc], wr=[r_pm])
        ph.op("dve", lambda e, l=l: e.tensor_tensor(out=modr[:], in0=pm[:, 0:144], in1=bad[:, l * 144:(l + 1) * 144],
                                                    op=ALU.add), rd=[r_pm, r_small], wr=[r_modr])
        for j in range(3):
            wj = 1.0 if j == 1 else 0.5
            o = l * 144
            sh = modr[:, (j * 3 + 0) * 16:(j * 3 + 1) * 16]
            scl = modr[:, (j * 3 + 1) * 16:(j * 3 + 2) * 16]
            gt = modr[:, (j * 3 + 2) * 16:(j * 3 + 3) * 16]
            ph.op("dve", lambda e, o=o, j=j, scl=scl, l=l: e.scalar_tensor_tensor(
                out=modo[:, o + j * 16:o + (j + 1) * 16], in0=scl, scalar=1.0, in1=gpr[:, l * 48 + j * 16:l * 48 + (j + 1) * 16],
                op0=ALU.add, op1=ALU.mult), rd=[r_modr, r_small], wr=[r_modo], strict=True)
            ph.op("dve", lambda e, o=o, j=j, sh=sh: e.tensor_copy(out=modo[:, o + 48 + j * 16:o + 48 + (j + 1) * 16], in_=sh),
                  rd=[r_modr], wr=[r_modo], strict=True)
            ph.op("dve", lambda e, o=o, j=j, gt=gt, wj=wj, l=l: e.scalar_tensor_tensor(
                out=modo[:, o + 96 + j * 16:o + 96 + (j + 1) * 16], in0=gt, scalar=wj, in1=gpo[:, l * 48 + j * 16:l * 48 + (j + 1) * 16],
                op0=ALU.mult, op1=ALU.mult), rd=[r_modr, r_small], wr=[r_modo], strict=True)
    ph.dma("sp", mods_d[:, :], modo[:], key="st", rd=[r_modo])
    ph.run()

    class TileCtx:
        def __init__(self, ph, with_xT=True, with_ring=True):
            self.ph = ph
            self.xT = ph.sb("xT", [128, KC, T], F32) if with_xT else None
            self.xo = ph.sb("xo", [128, KC, T], F32)
            self.mods = ph.sb("mods", [128, L * 144], F32)
            self.ones = ph.sb("ones", [128, 128], BF16)
            self.eps = ph.sb("eps", [128, 1], F32)
            self.sq = [ph.sb(f"sq{i}", [128, T], BF16) for i in range(2)]
            self.rstd = ph.sb("rstd", [128, T], F32)
            self.tmp = [ph.sb(f"tmp{i}", [128, T], F32) for i in range(2)]
            self.ring = [ph.sb(f"ring{i}", [128, T], F32) for i in range(4)] if with_ring else None
            self.pst = ph.ps("pst")
            self.r_x = [Res() for _ in range(KC)]
            self.r_xo = Res()
            self.r_mods = Res()
            self.r_const = Res()
            self.r_sq = [Res(), Res()]
            self.r_tmp = [Res(), Res()]
            self.r_ring = [Res() for _ in range(4)]
            self.r_rstd = Res()
            self.r_pst = Res()
            self.nring = 0
            ph.dma("sp", self.mods[:], mods_d[:, :], key="cst", wr=[self.r_mods])
            ph.op("dve", lambda e: e.memset(self.ones[:], 1.0), wr=[self.r_const])
            ph.op("dve", lambda e: e.memset(self.eps[:], EPS), wr=[self.r_const])

    def rms_stats(ph, tc, src_chunks, src_res, n_feat, cnt):
        n = len(src_chunks)
        for i, (ap, rr) in enumerate(zip(src_chunks, src_res)):
            b = cnt[0] % 2
            cnt[0] += 1
            ph.op("act", lambda e, ap=ap, b=b: e.activation(out=tc.sq[b][:], in_=ap, func=AF.Square),
                  rd=[rr], wr=[tc.r_sq[b]])
            ph.op("pe", lambda e, b=b, i=i: e.matmul(tc.pst[:], lhsT=tc.ones[:], rhs=tc.sq[b][:], start=(i == 0),
                                                     stop=(i == n - 1)), rd=[tc.r_sq[b], tc.r_const], wr=[tc.r_pst])
        ph.op("act", lambda e: e.activation(out=tc.rstd[:], in_=tc.pst[:], func=AF.Sqrt, scale=1.0 / n_feat, bias=tc.eps[:]),
              rd=[tc.r_pst, tc.r_const], wr=[tc.r_rstd])
        ph.op("dve", lambda e: e.reciprocal(out=tc.rstd[:], in_=tc.rstd[:]), rd=[tc.r_rstd], wr=[tc.r_rstd])

    def pre_norm(ph, tc, xn, r_xn, l, j, cnt, src=None, src_res=None):
        if src is None:
            src, src_res = tc.xT, tc.r_x
        rms_stats(ph, tc, [src[:, kc, :] for kc in range(KC)], src_res, float(D), cnt)
        o = l * 144
        for kc in range(KC):
            b = kc % 2
            ph.op("dve", lambda e, kc=kc, b=b: e.tensor_tensor(out=tc.tmp[b][:], in0=src[:, kc, :], in1=tc.rstd[:], op=ALU.mult),
                  rd=[src_res[kc], tc.r_rstd], wr=[tc.r_tmp[b]])
            ph.op("act", lambda e, kc=kc, b=b: e.activation(
                out=xn[:, kc, :], in_=tc.tmp[b][:], func=AF.Identity,
                scale=tc.mods[:, o + j * 16 + kc:o + j * 16 + kc + 1], bias=tc.mods[:, o + 48 + j * 16 + kc:o + 48 + j * 16 + kc + 1]),
                rd=[tc.r_tmp[b], tc.r_mods], wr=[r_xn])

    def post_stream(ph, tc, l, j, cnt, x_dram, t0, dres=None):
        rms_stats(ph, tc, [tc.xo[:, kc, :] for kc in range(KC)], [tc.r_xo] * KC, float(D), cnt)
        o = l * 144
        for kc in range(KC):
            sl = tc.nring % 4
            tc.nring += 1
            ph.dma("sp", tc.ring[sl][:], x_dram[kc * 128:(kc + 1) * 128, t0:t0 + T], key=f"xr{sl}", rd=([dres] if dres is not None else []),
                   wr=[tc.r_ring[sl]])
            ph.op("dve", lambda e, kc=kc: e.tensor_tensor(out=tc.xo[:, kc, :], in0=tc.xo[:, kc, :], in1=tc.rstd[:], op=ALU.mult),
                  rd=[tc.r_xo, tc.r_rstd], wr=[tc.r_xo])
            ph.op("dve", lambda e, kc=kc, sl=sl: e.scalar_tensor_tensor(
                out=tc.xo[:, kc, :], in0=tc.xo[:, kc, :], scalar=tc.mods[:, o + 96 + j * 16 + kc:o + 96 + j * 16 + kc + 1],
                in1=tc.ring[sl][:], op0=ALU.mult, op1=ALU.add), rd=[tc.r_xo, tc.r_ring[sl], tc.r_mods], wr=[tc.r_xo])

    def store_xo(ph, tc, dst, t0, dres=None):
        ph.dma("sp", dst[:, t0:t0 + T].rearrange("(c p) t -> p c t", p=128), tc.xo[:], key="xs", rd=[tc.r_xo],
               wr=([dres] if dres is not None else []))

    def load_x(ph, tc, src, t0):
        ph.dma("sp", tc.xT[:], src[:, t0:t0 + T].rearrange("(c p) t -> p c t", p=128), key="xl", wr=list(tc.r_x))

    class FFN:
        def __init__(self, ph, tc):
            self.ph, self.tc = ph, tc
            self.xn = ph.sb("xn", [128, KC, T], BF16)
            self.hT = ph.sb("hT", [128, HC, T], BF16)
            self.win = [ph.sb(f"win{i}", [128, KC, 256], BF16) for i in range(2)]
            self.wout = [ph.sb(f"wout{i}", [128, HC, 128], BF16) for i in range(2)]
            self.sg = [ph.sb(f"sg{i}", [128, T], F32) for i in range(2)]
            self.pu = [ph.ps(f"pu{i}") for i in range(2)]
            self.pg = [ph.ps(f"pg{i}") for i in range(2)]
            self.po = [ph.ps(f"po{i}") for i in range(2)]
            self.r_xn = Res()
            self.r_h = [Res() for _ in range(HC)]
            self.r_win = [Res(), Res()]
            self.r_wout = [Res(), Res()]
            self.r_sg = [Res(), Res()]
            self.r_pu = [Res(), Res()]
            self.r_pg = [Res(), Res()]
            self.r_po = [Res(), Res()]
            self.nin = 0
            self.nout = 0
            self.nj = 0

        def tile_in(self, l, which, hook=None):
            ph = self.ph
            for hc in range(HC):
                s = self.nin % 2
                self.nin += 1
                ph.dma("sp", self.win[s][:], win_b[which][l, hc], key=f"win{s}", wr=[self.r_win[s]])
                b = self.nj % 2
                self.nj += 1
                for kc in range(KC):
                    ph.op("pe", lambda e, s=s, kc=kc, b=b: e.matmul(
                        self.pu[b][:], lhsT=self.win[s][:, kc, 0:128], rhs=self.xn[:, kc, :],
                        start=(kc == 0), stop=(kc == KC - 1)), rd=[self.r_win[s], self.r_xn], wr=[self.r_pu[b]])
                for kc in range(KC):
                    ph.op("pe", lambda e, s=s, kc=kc, b=b: e.matmul(
                        self.pg[b][:], lhsT=self.win[s][:, kc, 128:256], rhs=self.xn[:, kc, :],
                        start=(kc == 0), stop=(kc == KC - 1)), rd=[self.r_win[s], self.r_xn], wr=[self.r_pg[b]])
                ph.op("act", lambda e, b=b: e.activation(out=self.sg[b][:], in_=self.pg[b][:], func=AF.Silu),
                      rd=[self.r_pg[b]], wr=[self.r_sg[b]])
                ph.op("dve", lambda e, b=b, hc=hc: e.tensor_tensor(out=self.hT[:, hc, :], in0=self.pu[b][:], in1=self.sg[b][:],
                                                                  op=ALU.mult),
                      rd=[self.r_pu[b], self.r_sg[b]], wr=[self.r_h[hc]])
                if hook is not None and hc == 2:
                    hook()

        def tile_out(self, l, which, hook=None):
            ph, tc = self.ph, self.tc
            for dc in range(KC):
                s = self.nout % 2
                self.nout += 1
                ph.dma("sp", self.wout[s][:], wout_b[which][l, dc], key=f"wout{s}", wr=[self.r_wout[s]])
                for hc in range(HC):
                    ph.op("pe", lambda e, s=s, hc=hc: e.matmul(
                        self.po[s][:], lhsT=self.wout[s][:, hc, :], rhs=self.hT[:, hc, :], start=(hc == 0), stop=(hc == HC - 1)),
                        rd=[self.r_wout[s], self.r_h[hc]], wr=[self.r_po[s]])
                ph.op("act", lambda e, s=s, dc=dc: e.activation(out=tc.xo[:, dc, :], in_=self.po[s][:], func=AF.Copy),
                      rd=[self.r_po[s]], wr=[tc.r_xo])
                if hook is not None and dc == 3:
                    hook()

    for l in range(L):
        x_src = xT_in if l == 0 else xB
        ph = Phase(nc, f"f1_{l}")
        tc = TileCtx(ph)
        ffn = FFN(ph, tc)
        cnt = [0]
        load_x(ph, tc, x_src, 0)
        pre_norm(ph, tc, ffn.xn, ffn.r_xn, l, 0, cnt)
        for ti in range(NT):
            nxt = ti + 1 < NT
            ffn.tile_in(l, 0, hook=(lambda ti=ti: load_x(ph, tc, x_src, (ti + 1) * T)) if nxt else None)
            ffn.tile_out(l, 0, hook=(lambda: pre_norm(ph, tc, ffn.xn, ffn.r_xn, l, 0, cnt)) if nxt else None)
            post_stream(ph, tc, l, 0, cnt, x_src, ti * T)
            store_xo(ph, tc, xA, ti * T)
        ph.run()

        ph = Phase(nc, f"mi_{l}")
        tc = TileCtx(ph, with_ring=False)
        xn = ph.sb("xn", [128, KC, T], BF16)
        r_xn = Res()
        wfm = [ph.sb(f"wfm{i}", [128, 4, KC, 128], BF16) for i in range(2)]
        r_wfm = [Res(), Res()]
        wtk = ph.sb("wtk", [128, KC, 768], BF16)
        r_wtk = Res()
        stg_u = ph.sb("stg_u", [128, 4, T], BF16)
        stg_f = [ph.sb(f"stg_f{i}", [128, 4, T], F32) for i in range(2)]
        sig = ph.sb("sig", [128, T], F32)
        stg_a = ph.sb("stg_a", [64, T], F32)
        stg_k = ph.sb("stg_k", [128, 4, 256], F32)
        stg_v = ph.sb("stg_v", [128, 4, 512], BF16)
        pp = [ph.ps(f"pp{i}") for i in range(3)]
        pk = ph.ps("pk")
        pv = ph.ps("pv")
        r_pp = [Res() for _ in range(3)]
        r_pk, r_pv = Res(), Res()
        r_su, r_sf, r_sig, r_sa, r_sk, r_sv = Res(), [Res(), Res()], Res(), Res(), Res(), Res()
        ph.dma("sp", wtk[:], wmt_b[l], key="wtk", wr=[r_wtk])
        cnt = [0]
        nw = 0
        npp = 0
        nsf = 0
        for ti in range(NT):
            t0 = ti * T
            load_x(ph, tc, xA, t0)
            pre_norm(ph, tc, xn, r_xn, l, 1, cnt)
            cur_sf = None
            for ch in range(NCH_FM):
                if ch % 4 == 0:
                    s = nw % 2
                    nw += 1
                    n_in = min(4, NCH_FM - ch)
                    ph.dma("sp", wfm[s][:, 0:n_in], wmi_b[l, ch:ch + n_in].rearrange("g p k n -> p g k n"), key=f"wfm{s}",
                           wr=[r_wfm[s]])
                b = npp % 3
                npp += 1
                M = 64 if ch == CH_A else 128
                for kc in range(KC):
                    ph.op("pe", lambda e, s=s, ch=ch, kc=kc, b=b, M=M: e.matmul(
                        pp[b][0:M, :], lhsT=wfm[s][:, ch % 4, kc, 0:M], rhs=xn[:, kc, :], start=(kc == 0), stop=(kc == KC - 1)),
                        rd=[r_wfm[s], r_xn], wr=[r_pp[b]])
                i4 = ch % 4
                if ch < CH_Q:
                    ph.op("act", lambda e, b=b, i4=i4: e.activation(out=stg_u[:, i4, :], in_=pp[b][:], func=AF.Copy),
                          rd=[r_pp[b]], wr=[r_su])
                    if i4 == 3:
                        ph.dma("sp", UT[:, t0:t0 + T].rearrange("(c p) t -> p c t", p=128), stg_u[:], key="su", rd=[r_su])
                elif ch < CH_C:
                    if (ch >= CH_OG and i4 == 0) or ch == CH_Q:
                        sfi = nsf % 2
                        nsf += 1
                    if ch < CH_OG:
                        slot = ch - CH_Q
                        func = AF.Copy
                    else:
                        slot = i4
                        func = AF.Silu if ch < CH_RIN else (AF.Copy if ch < CH_RG else AF.Gelu)
                    ph.op("act", lambda e, b=b, slot=slot, sfi=sfi, func=func: e.activation(
                        out=stg_f[sfi][:, slot, :], in_=pp[b][:], func=func), rd=[r_pp[b]], wr=[r_sf[sfi]])
                    if ch == CH_K + 1:
                        ph.dma("sp", QT[:, t0:t0 + T].rearrange("(c p) t -> p c t", p=128), stg_f[sfi][:, 0:2, :], key=f"sf{sfi}",
                               rd=[r_sf[sfi]])
                        ph.dma("sp", KT[:, t0:t0 + T].rearrange("(c p) t -> p c t", p=128), stg_f[sfi][:, 2:4, :], key=f"sf{sfi}",
                               rd=[r_sf[sfi]])
                    elif ch >= CH_OG and i4 == 3:
                        if ch < CH_RIN:
                            dst = SOG[:, t0:t0 + T]
                        elif ch < CH_RG:
                            dst = RIN[:, 3 + t0:3 + t0 + T]
                        else:
                            dst = GR[:, t0:t0 + T]
                        ph.dma("sp", dst.rearrange("(c p) t -> p c t", p=128), stg_f[sfi][:], key=f"sf{sfi}", rd=[r_sf[sfi]])
                elif ch < CH_A:
                    ci = (ch - CH_C) // 2
                    if (ch - CH_C) % 2 == 0:
                        if ci == 0:
                            sfi = nsf % 2
                            nsf += 1
                        ph.op("act", lambda e, b=b: e.activation(out=sig[:], in_=pp[b][:], func=AF.Sigmoid),
                              rd=[r_pp[b]], wr=[r_sig])
                    else:
                        ph.op("dve", lambda e, b=b, ci=ci, sfi=sfi: e.tensor_tensor(out=stg_f[sfi][:, ci, :], in0=pp[b][:], in1=sig[:],
                                                                                    op=ALU.mult),
                              rd=[r_pp[b], r_sig], wr=[r_sf[sfi]])
                        if ci == 3:
                            ph.dma("sp", UC[:, 15 + t0:15 + t0 + T].rearrange("(c p) t -> p c t", p=128), stg_f[sfi][:],
                                   key=f"sf{sfi}", rd=[r_sf[sfi]])
                else:
                    ph.op("act", lambda e, b=b: e.activation(out=stg_a[:], in_=pp[b][0:64, :], func=AF.Copy),
                          rd=[r_pp[b]], wr=[r_sa])
                    ph.dma("sp", AT[:, t0:t0 + T], stg_a[:], key="sa", rd=[r_sa])
            for sub in range(4):
                for kc in range(KC):
                    ph.op("pe", lambda e, sub=sub, kc=kc: e.matmul(
                        pk[:, 0:256], lhsT=xn[:, kc, sub * 128:(sub + 1) * 128], rhs=wtk[:, kc, 0:256], start=(kc == 0),
                        stop=(kc == KC - 1)), rd=[r_xn, r_wtk], wr=[r_pk])
                for kc in range(KC):
                    ph.op("pe", lambda e, sub=sub, kc=kc: e.matmul(
                        pv[:], lhsT=xn[:, kc, sub * 128:(sub + 1) * 128], rhs=wtk[:, kc, 256:768], start=(kc == 0),
                        stop=(kc == KC - 1)), rd=[r_xn, r_wtk], wr=[r_pv])
                ph.op("dve", lambda e, sub=sub: e.tensor_copy(out=stg_k[:, sub, :], in_=pk[:, 0:256]), rd=[r_pk], wr=[r_sk])
                ph.op("act", lambda e, sub=sub: e.activation(out=stg_v[:, sub, :], in_=pv[:], func=AF.Copy), rd=[r_pv], wr=[r_sv])
            ph.dma("sp", KTOK[t0:t0 + T, :].rearrange("(s p) n -> p s n", p=128), stg_k[:], key="sk", rd=[r_sk])
            ph.dma("sp", VTOK[t0:t0 + T, :].rearrange("(s p) n -> p s n", p=128), stg_v[:], key="sv", rd=[r_sv])
        ph.run()

        ph = Phase(nc, f"fo_{l}")
        uT = ph.sb("uT", [128, S], BF16)
        Z = ph.sb("Z", [128, 2, 128, N2], BF16)
        cs_f = ph.sb("cs_f", [128, 256], F32)
        cs = ph.sb("cs", [128, 256], BF16)
        sa_f = ph.sb("sa_f", [128, 2, 256], F32)
        sa = ph.sb("sa", [128, 2, 256], BF16)
        tw = ph.sb("tw", [128, 2, 128], F32)
        bd_f = ph.sb("bd_f", [128, 2, 128], F32)
        bd = ph.sb("bd", [128, 2, 128], BF16)
        NPB = 4
        apr = [ph.sb(f"apr{i}", [128, NPB, 128], BF16) for i in range(2)]
        api = [ph.sb(f"api{i}", [128, NPB, 128], BF16) for i in range(2)]
        t1 = ph.sb("t1", [128, 2, 128], F32)
        t2 = ph.sb("t2", [128, 2, 128], F32)
        yo = [ph.sb(f"yo{i}", [128, NPB, 128], BF16) for i in range(2)]
        pz = [ph.ps(f"pz{i}") for i in range(2)]
        pa = [ph.ps(f"pa{i}") for i in range(2)]
        pb = [ph.ps(f"pb{i}") for i in range(2)]
        r_c, r_u, r_Z = Res(), Res(), Res()
        r_pz, r_pa, r_pb = [Res(), Res()], [Res(), Res()], [Res(), Res()]
        r_ap, r_t1, r_t2, r_yo = [Res(), Res()], Res(), Res(), [Res(), Res()]
        ph.dma("sp", cs_f[:], c_cs128[:, :], key="c", wr=[r_c])
        ph.dma("sp", sa_f[:], c_sa.rearrange("a p n -> p a n"), key="c", wr=[r_c])
        ph.dma("sp", tw[:], c_tw.rearrange("a p n -> p a n"), key="c", wr=[r_c])
        ph.dma("sp", bd_f[:], c_bd.rearrange("a p n -> p a n"), key="c", wr=[r_c])
        ph.op("dve", lambda e: e.tensor_copy(out=cs[:], in_=cs_f[:]), rd=[r_c], wr=[r_c])
        ph.op("dve", lambda e: e.tensor_copy(out=sa[:], in_=sa_f[:]), rd=[r_c], wr=[r_c])
        ph.op("dve", lambda e: e.tensor_copy(out=bd[:], in_=bd_f[:]), rd=[r_c], wr=[r_c])
        nz = na = nb = 0
        for g in range(4):
            ph.dma("sp", uT[:], UT[g * 128:(g + 1) * 128, :], key="u", wr=[r_u])
            for s2 in range(0, N2, 2):
                b = nz % 2
                nz += 1
                for q in range(2):
                    ph.op("pe", lambda e, s2=s2, q=q, b=b: e.matmul(
                        pz[b][:, q * 256:(q + 1) * 256], lhsT=uT[:, s2 + q::N2], rhs=cs[:], start=True, stop=True),
                        rd=[r_u, r_c], wr=[r_pz[b]])
                for q in range(2):
                    ph.op("act", lambda e, s2=s2, b=b, q=q: e.activation(
                        out=Z[:, :, :, s2 + q], in_=pz[b][:, q * 256:(q + 1) * 256].rearrange("p (c n) -> p c n", c=2), func=AF.Copy),
                        rd=[r_pz[b]], wr=[r_Z])
            nsets = 128 // CHP
            for cs0 in range(0, nsets, NPB):
                ab = na % 2
                for q in range(NPB):
                    c0 = (cs0 + q) * CHP
                    b = na % 2
                    na += 1
                    zr_ap = Z[:, 0, c0:c0 + CHP, :].rearrange("p j s -> p (j s)")
                    zi_ap = Z[:, 1, c0:c0 + CHP, :].rearrange("p j s -> p (j s)")
                    ph.op("pe", lambda e, b=b, zr_ap=zr_ap: e.matmul(pa[b][:, 0:256], lhsT=zr_ap, rhs=sa[:, 0, :], start=True, stop=False),
                          rd=[r_Z, r_c], wr=[r_pa[b]])
                    ph.op("pe", lambda e, b=b, zi_ap=zi_ap: e.matmul(pa[b][:, 0:256], lhsT=zi_ap, rhs=sa[:, 1, :], start=False, stop=True),
                          rd=[r_Z, r_c], wr=[r_pa[b]])
                    ph.op("dve", lambda e, b=b: e.tensor_tensor(out=t1[:, 0, :], in0=pa[b][:, 0:128], in1=tw[:, 0, :], op=ALU.mult),
                          rd=[r_pa[b], r_c], wr=[r_t1])
                    ph.op("dve", lambda e, b=b: e.tensor_tensor(out=t1[:, 1, :], in0=pa[b][:, 128:256], in1=tw[:, 1, :], op=ALU.mult),
                          rd=[r_pa[b], r_c], wr=[r_t1])
                    ph.op("dve", lambda e, b=b: e.tensor_tensor(out=t2[:, 0, :], in0=pa[b][:, 0:128], in1=tw[:, 1, :], op=ALU.mult),
                          rd=[r_pa[b], r_c], wr=[r_t2])
                    ph.op("dve", lambda e, b=b: e.tensor_tensor(out=t2[:, 1, :], in0=pa[b][:, 128:256], in1=tw[:, 0, :], op=ALU.mult),
                          rd=[r_pa[b], r_c], wr=[r_t2])
                    sb_ = (cs0 // NPB) % 2
                    ph.op("pool", lambda e, q=q, sb_=sb_: e.tensor_tensor(out=apr[sb_][:, q, :], in0=t1[:, 0, :], in1=t1[:, 1, :],
                                                                          op=ALU.subtract), rd=[r_t1], wr=[r_ap[sb_]])
                    ph.op("pool", lambda e, q=q, sb_=sb_: e.tensor_tensor(out=api[sb_][:, q, :], in0=t2[:, 0, :], in1=t2[:, 1, :],
                                                                          op=ALU.add), rd=[r_t2], wr=[r_ap[sb_]])
                sb_ = (cs0 // NPB) % 2
                b = nb % 2
                nb += 1
                ph.op("pe", lambda e, b=b, sb_=sb_: e.matmul(pb[b][:], lhsT=bd[:, 0, :], rhs=apr[sb_][:].rearrange("p q n -> p (q n)"),
                                                            start=True, stop=False), rd=[r_ap[sb_], r_c], wr=[r_pb[b]])
                ph.op("pe", lambda e, b=b, sb_=sb_: e.matmul(pb[b][:], lhsT=bd[:, 1, :], rhs=api[sb_][:].rearrange("p q n -> p (q n)"),
                                                            start=False, stop=True), rd=[r_ap[sb_], r_c], wr=[r_pb[b]])
                ph.op("act", lambda e, b=b: e.activation(out=yo[b][:].rearrange("p q n -> p (q n)"), in_=pb[b][:], func=AF.Copy,
                                                         scale=1.0 / math.sqrt(S * 128.0)), rd=[r_pb[b]], wr=[r_yo[b]])
                row0 = g * 128 + cs0 * CHP
                dst = Y[row0:row0 + NPB * CHP, :].rearrange("(q j) (k2 k1) -> (j k2) q k1", q=NPB, k1=128)
                ph.dma("sp", dst, yo[b][:], key=f"yo{b}", rd=[r_yo[b]])
        ph.run()

        ph = Phase(nc, f"gl_{l}")
        SBK = 4
        wal_f = ph.sb("wal", [64, 2, 256], F32)
        bal_f = ph.sb("bal", [64, 2, 256], F32)
        gn = ph.sb("gn", [128, L], F32)
        ones = ph.sb("ones", [128, 128], F32)
        epst = ph.sb("eps", [128, 1], F32)
        ident = ph.sb("ident", [128, 128], F32)
        tri = ph.sb("tri", [128, 4, 128], F32)
        msk = ph.sb("msk", [128, 2, 128], F32)
        at_sb = ph.sb("at", [64, S], F32)
        qt_sb = [ph.sb(f"qt{i}", [128, 2, SBK * 128], F32) for i in range(2)]
        kt_sb = [ph.sb(f"kt{i}", [128, 2, SBK * 128], F32) for i in range(2)]
        ktok_sb = [ph.sb(f"ktok{i}", [128, SBK, 256], F32) for i in range(2)]
        vtok_sb = [ph.sb(f"vtok{i}", [128, SBK, 512], BF16) for i in range(2)]
        of_sb = [ph.sb(f"of{i}", [128, SBK, 512], F32) for i in range(2)]
        sog_sb = [ph.sb(f"sog{i}", [128, 4, SBK * 128], F32) for i in range(2)]
        yg_sb = [ph.sb(f"yg{i}", [128, 4, SBK * 128], BF16) for i in range(2)]
        ez = ph.sb("ez", [128, 256], F32)
        sp_ = ph.sb("sp", [128, 256], F32)
        btok = ph.sb("btok", [128, 256], F32)
        dtok = ph.sb("dtok", [128, 256], F32)
        khat = ph.sb("khat", [128, 256], BF16)
        E = ph.sb("E", [128, 2, 128], F32)
        Ei = ph.sb("Ei", [128, 2, 128], F32)
        qtl = ph.sb("qtl", [128, 2, 128], BF16)
        ktl = ph.sb("ktl", [128, 2, 128], BF16)
        scm = [ph.sb(f"scm{i}", [128, 128], BF16) for i in range(2)]
        st_f = ph.sb("st_f", [128, 2, 128], F32)
        st_b = ph.sb("st_b", [128, 4, 128], BF16)
        osum = ph.sb("osum", [128, 512], F32)
        sqo = ph.sb("sqo", [128, 512], F32)
        rsd = ph.sb("rsd", [128, 512], F32)
        yt = ph.sb("yt", [128, 512], F32)
        p_z = ph.ps("p_z")
        p_b = ph.ps("p_b")
        p_sc = [ph.ps(f"p_sc{i}") for i in range(2)]
        p_o = ph.ps("p_o")
        p_st = ph.ps("p_st")
        p_ot = ph.ps("p_ot")
        p_ss = ph.ps("p_ss")
        R = {n: Res() for n in ["c", "at", "ez", "sp", "btok", "dtok", "khat", "E", "Ei", "qtl", "ktl", "stf", "stb", "osum", "sqo",
                                "rsd", "yt", "pz", "pbf", "pb", "pdr", "po", "pst", "pot", "pss"]}
        r_scm, r_psc = [Res(), Res()], [Res(), Res()]
        r_q, r_k, r_kt, r_v, r_of, r_sog, r_yg = ([Res(), Res()] for _ in range(7))
        ph.dma("sp", wal_f[:], walpha[:, l * 512:(l + 1) * 512].rearrange("r (d n) -> r d n", d=2), key="c", wr=[R["c"]])
        ph.dma("sp", bal_f[:], balpha[:, l * 512:(l + 1) * 512].rearrange("r (d n) -> r d n", d=2), key="c", wr=[R["c"]])
        ph.dma("sp", gn[:], gnorm[:, :], key="c", wr=[R["c"]])
        ph.dma("sp", ident[:], c_ident[:, :], key="c", wr=[R["c"]])
        ph.dma("sp", tri[:], c_tri.rearrange("a p n -> p a n"), key="c", wr=[R["c"]])
        ph.dma("sp", msk[:], c_mask.rearrange("a p n -> p a n"), key="c", wr=[R["c"]])
        ph.dma("sp", at_sb[:], AT[:, :], key="at", wr=[R["at"]])
        ph.op("dve", lambda e: e.memset(ones[:], 1.0), wr=[R["c"]])
        ph.op("dve", lambda e: e.memset(epst[:], EPS), wr=[R["c"]])
        nsb = 0
        nsc = 0
        for d in range(2):
            ph.op("dve", lambda e: e.memset(st_f[:], 0.0), rd=[R["stf"]], wr=[R["stf"]])
            ph.op("dve", lambda e: e.memset(st_b[:], 0.0), rd=[R["stb"]], wr=[R["stb"]])
            blocks = list(range(NB)) if d == 0 else list(range(NB - 1, -1, -1))
            for bi, blk in enumerate(blocks):
                sbk = blk // SBK
                ib = blk % SBK
                if bi % SBK == 0:
                    sbuf_i = nsb % 2
                    nsb += 1
                    c0 = sbk * SBK * 128
                    c1 = c0 + SBK * 128
                    ph.dma("sp", qt_sb[sbuf_i][:], QT[:, c0:c1].rearrange("(c p) t -> p c t", p=128), key=f"q{sbuf_i}",
                           wr=[r_q[sbuf_i]])
                    ph.dma("sp", kt_sb[sbuf_i][:], KT[:, c0:c1].rearrange("(c p) t -> p c t", p=128), key=f"k{sbuf_i}",
                           wr=[r_k[sbuf_i]])
                    ph.dma("sp", ktok_sb[sbuf_i][:], KTOK[c0:c1, :].rearrange("(s p) n -> p s n", p=128), key=f"kt{sbuf_i}",
                           wr=[r_kt[sbuf_i]])
                    ph.dma("sp", vtok_sb[sbuf_i][:], VTOK[c0:c1, :].rearrange("(s p) n -> p s n", p=128), key=f"v{sbuf_i}",
                           wr=[r_v[sbuf_i]])
                    if d == 1:
                        ph.dma("sp", of_sb[sbuf_i][:], OF[c0:c1, :].rearrange("(s p) n -> p s n", p=128), key=f"of{sbuf_i}",
                               wr=[r_of[sbuf_i]])
                        ph.dma("sp", sog_sb[sbuf_i][:], SOG[:, c0:c1].rearrange("(c p) t -> p c t", p=128), key=f"sg{sbuf_i}",
                               wr=[r_sog[sbuf_i]])
                si = sbuf_i
                tk0 = blk * 128
                cl = slice(ib * 128, (ib + 1) * 128)
                ph.op("pe", lambda e, d=d, tk0=tk0: e.matmul(p_z[:, 0:256], lhsT=at_sb[32 * d:32 * d + 16, tk0:tk0 + 128],
                                                             rhs=wal_f[32 * d:32 * d + 16, d, :], start=True, stop=False),
                      rd=[R["at"], R["c"]], wr=[R["pz"]])
                ph.op("pe", lambda e, d=d: e.matmul(p_z[:, 0:256], lhsT=ones[32 * d:32 * d + 1, :], rhs=bal_f[32 * d:32 * d + 1, d, :], start=False, stop=True),
                      rd=[R["c"]], wr=[R["pz"]])
                ph.op("act", lambda e: e.activation(out=ez[:], in_=p_z[:, 0:256], func=AF.Exp, scale=-1.0), rd=[R["pz"]], wr=[R["ez"], R["pz"]])
                ph.op("act", lambda e: e.activation(out=sp_[:], in_=ez[:], func=AF.Ln, bias=1.0), rd=[R["ez"]], wr=[R["sp"]])
                ph.op("pe", lambda e, d=d: e.matmul(p_b[:, 0:256], lhsT=tri[:, 2 * d, :], rhs=sp_[:], start=True, stop=True),
                      rd=[R["sp"], R["c"]], wr=[R["pb"]])
                ph.op("pe", lambda e, d=d: e.matmul(p_b[:, 256:512], lhsT=tri[:, 2 * d + 1, :], rhs=sp_[:], start=True, stop=True),
                      rd=[R["sp"], R["c"]], wr=[R["pb"]])
                ph.op("dve", lambda e: e.tensor_copy(out=btok[:], in_=p_b[:, 0:256]), rd=[R["pb"]], wr=[R["btok"], R["pb"]])
                ph.op("act", lambda e: e.activation(out=dtok[:], in_=p_b[:, 256:512], func=AF.Exp), rd=[R["pb"]], wr=[R["dtok"], R["pb"]])
                ph.op("dve", lambda e, si=si, ib=ib: e.tensor_tensor(out=khat[:], in0=ktok_sb[si][:, ib, :], in1=dtok[:], op=ALU.mult),
                      rd=[R["dtok"], r_kt[si]], wr=[R["khat"]])
                for pr in range(2):
                    ph.op("pe", lambda e, pr=pr: e.transpose(out=p_z[:, 256 + pr * 128:256 + (pr + 1) * 128],
                                                             in_=btok[:, pr * 128:(pr + 1) * 128], identity=ident[:]),
                          rd=[R["btok"], R["c"]], wr=[R["pz"]])
                ph.op("act", lambda e: e.activation(out=E[:].rearrange("p a n -> p (a n)"), in_=p_z[:, 256:512], func=AF.Exp),
                      rd=[R["pz"]], wr=[R["E"], R["pz"]])
                ph.op("act", lambda e: e.activation(out=Ei[:].rearrange("p a n -> p (a n)"), in_=p_z[:, 256:512], func=AF.Exp, scale=-1.0),
                      rd=[R["pz"]], wr=[R["Ei"], R["pz"]])
                ph.op("dve", lambda e, si=si, cl=cl: e.scalar_tensor_tensor(out=qtl[:], in0=qt_sb[si][:, :, cl], scalar=0.125, in1=E[:],
                                                                            op0=ALU.mult, op1=ALU.mult),
                      rd=[R["E"], r_q[si]], wr=[R["qtl"]])
                ph.op("pool", lambda e, si=si, cl=cl: e.tensor_tensor(out=ktl[:], in0=kt_sb[si][:, :, cl], in1=Ei[:], op=ALU.mult),
                      rd=[R["Ei"], r_k[si]], wr=[R["ktl"]])
                for h in range(4):
                    pr, base = h // 2, (h % 2) * 64
                    b = nsc % 2
                    nsc += 1
                    ph.op("pe", lambda e, pr=pr, base=base, b=b: e.matmul(
                        p_sc[b][:, 0:128], lhsT=ktl[base:base + 64, pr, :], rhs=qtl[base:base + 64, pr, :], start=True, stop=True),
                        rd=[R["ktl"], R["qtl"]], wr=[r_psc[b]])
                    ph.op("dve", lambda e, b=b, d=d: e.tensor_tensor(out=scm[b][:], in0=p_sc[b][:, 0:128], in1=msk[:, d, :], op=ALU.mult),
                          rd=[r_psc[b], R["c"]], wr=[r_scm[b]])
                    ph.op("pe", lambda e, b=b, h=h, si=si, ib=ib: e.matmul(
                        p_o[:, h * 128:(h + 1) * 128], lhsT=scm[b][:], rhs=vtok_sb[si][:, ib, h * 128:(h + 1) * 128], start=True,
                        stop=False), rd=[r_scm[b], r_v[si]], wr=[R["po"]])
                    ph.op("pe", lambda e, h=h, pr=pr, base=base: e.matmul(
                        p_o[:, h * 128:(h + 1) * 128], lhsT=qtl[:, pr, :], rhs=st_b[:, h, :], start=False,
                        stop=True), rd=[R["qtl"], R["stb"]], wr=[R["po"]])
                for h in range(4):
                    pr, base = h // 2, (h % 2) * 64
                    ph.op("pe", lambda e, h=h, pr=pr, base=base, si=si, ib=ib: e.matmul(
                        p_st[base:base + 64, pr * 128:(pr + 1) * 128], lhsT=khat[:, h * 64:(h + 1) * 64],
                        rhs=vtok_sb[si][:, ib, h * 128:(h + 1) * 128], start=True, stop=True),
                        rd=[R["khat"], r_v[si]], wr=[R["pst"]])
                tl = 127 if d == 0 else 0
                for pr in range(2):
                    ph.op("dve", lambda e, pr=pr, tl=tl: e.scalar_tensor_tensor(
                        out=st_f[:, pr, :], in0=st_f[:, pr, :], scalar=E[:, pr, tl:tl + 1], in1=p_st[:, pr * 128:(pr + 1) * 128],
                        op0=ALU.mult, op1=ALU.add), rd=[R["pst"], R["E"], R["stf"]], wr=[R["stf"]])
                for hh in range(2):
                    ph.op("pool", lambda e, hh=hh: e.tensor_copy(out=st_b[hh * 64:(hh + 1) * 64, hh::2, :],
                                                                 in_=st_f[hh * 64:(hh + 1) * 64, :, :]), rd=[R["stf"]], wr=[R["stb"]])
                if d == 0:
                    ph.op("act", lambda e, si=si, ib=ib: e.activation(out=of_sb[si][:, ib, :], in_=p_o[:], func=AF.Copy),
                          rd=[R["po"]], wr=[r_of[si]])
                    if bi % SBK == SBK - 1:
                        c0 = sbk * SBK * 128
                        ph.dma("sp", OF[c0:c0 + SBK * 128, :].rearrange("(s p) n -> p s n", p=128), of_sb[si][:], key=f"o{si}",
                               rd=[r_of[si]])
                else:
                    ph.op("dve", lambda e, si=si, ib=ib: e.tensor_tensor(out=osum[:], in0=p_o[:], in1=of_sb[si][:, ib, :], op=ALU.add),
                          rd=[R["po"], r_of[si]], wr=[R["osum"]])
                    for h in range(4):
                        ph.op("pe", lambda e, h=h: e.transpose(out=p_ot[:, h * 128:(h + 1) * 128], in_=osum[:, h * 128:(h + 1) * 128],
                                                               identity=ident[:]), rd=[R["osum"], R["c"]], wr=[R["pot"]])
                    ph.op("act", lambda e: e.activation(out=sqo[:], in_=p_ot[:], func=AF.Square), rd=[R["pot"]], wr=[R["sqo"], R["pot"]])
                    ph.op("pe", lambda e: e.matmul(p_ss[:], lhsT=ones[:], rhs=sqo[:], start=True, stop=True),
                          rd=[R["sqo"], R["c"]], wr=[R["pss"]])
                    ph.op("act", lambda e: e.activation(out=rsd[:], in_=p_ss[:], func=AF.Sqrt, scale=1.0 / 128.0, bias=epst[:]),
                          rd=[R["pss"], R["c"]], wr=[R["rsd"]])
                    ph.op("dve", lambda e: e.reciprocal(out=rsd[:], in_=rsd[:]), rd=[R["rsd"]], wr=[R["rsd"]])
                    ph.op("dve", lambda e: e.scalar_tensor_tensor(out=yt[:], in0=p_ot[:], scalar=gn[:, l:l + 1], in1=rsd[:], op0=ALU.mult,
                                                                  op1=ALU.mult), rd=[R["pot"], R["rsd"], R["c"]], wr=[R["yt"], R["pot"]])
                    ph.op("pool", lambda e, si=si, cl=cl: e.tensor_tensor(
                        out=yg_sb[si][:, :, cl], in0=yt[:].rearrange("p (h t) -> p h t", h=4), in1=sog_sb[si][:, :, cl], op=ALU.mult),
                        rd=[R["yt"], r_sog[si]], wr=[r_yg[si]])
                    if bi % SBK == SBK - 1:
                        c0 = sbk * SBK * 128
                        ph.dma("sp", Y[512:1024, c0:c0 + SBK * 128].rearrange("(c p) t -> p c t", p=128), yg_sb[si][:], key=f"o{si}",
                               rd=[r_yg[si]])
        ph.run()

        ph = Phase(nc, f"lr_{l}")
        TTL = min(1024, S)
        NTTL = S // TTL
        cw = ph.sb("cw", [128, 32], F32)
        cb = ph.sb("cb", [128, 8], F32)
        ba = ph.sb("ba", [128, 8], F32)
        bx = ph.sb("bx", [128, 8], F32)
        lam = ph.sb("lam", [128, 8], F32)
        cl1 = ph.sb("cl1", [128, 8], F32)
        cl2 = ph.sb("cl2", [128, 8], F32)
        zero1 = ph.sb("zero1", [128, 1], F32)
        wa_f = ph.sb("wa_f", [128, 8, 128], F32)
        wx_f = ph.sb("wx_f", [128, 8, 128], F32)
        wa_b = ph.sb("wa_b", [128, 8, 128], BF16)
        wx_b = ph.sb("wx_b", [128, 8, 128], BF16)
        xin = [ph.sb(f"xin{i}", [128, TTL + 3], F32) for i in range(2)]
        xc_2 = [ph.sb(f"xc{i_}", [128, TTL], F32) for i_ in range(2)]
        xcb_2 = [ph.sb(f"xcb{i_}", [128, TTL], BF16) for i_ in range(2)]
        rg_2 = [ph.sb(f"rg{i_}", [128, TTL], F32) for i_ in range(2)]
        ig_2 = [ph.sb(f"ig{i_}", [128, TTL], F32) for i_ in range(2)]
        av_2 = [ph.sb(f"av{i_}", [128, TTL], F32) for i_ in range(2)]
        a2_2 = [ph.sb(f"a2{i_}", [128, TTL], F32) for i_ in range(2)]
        uu_2 = [ph.sb(f"uu{i_}", [128, TTL], F32) for i_ in range(2)]
        hh = [ph.sb(f"hh{i}", [128, TTL], F32) for i in range(2)]
        hf_2 = [ph.sb(f"hf{i_}", [128, TTL], F32) for i_ in range(2)]
        grt_2 = [ph.sb(f"grt{i_}", [128, TTL], F32) for i_ in range(2)]
        yr = [ph.sb(f"yr{i}", [128, TTL], BF16) for i in range(2)]
        carry = ph.sb("carry", [128, 1], F32)
        p_r = [ph.ps(f"p_r{i}") for i in range(2)]
        p_i = [ph.ps(f"p_i{i}") for i in range(2)]
        R_2 = [{n: Res() for n in ["xc", "xcb", "rg", "ig", "av", "a2", "uu", "hf", "grt"]} for _ in range(2)]
        R_c = {"c": Res(), "carry": Res()}
        nit = [0]
        r_xin, r_hh, r_yr, r_pr, r_pi = ([Res(), Res()] for _ in range(5))
        o8 = l * 8
        ph.op("dve", lambda e: e.memset(zero1[:], 0.0), wr=[R_c["c"]])
        ph.dma("sp", cw[:], lcw[:, l * 32:(l + 1) * 32], key="c", wr=[R_c["c"]])
        for tns, src in ((cb, lcb), (ba, lba), (bx, lbx), (lam, llam)):
            ph.dma("sp", tns[:], src[:, o8:o8 + 8], key="c", wr=[R_c["c"]])
        ph.dma("sp", wa_f[:], lwa[o8:o8 + 8].rearrange("a p n -> p a n"), key="c", wr=[R_c["c"]])
        ph.dma("sp", wx_f[:], lwx[o8:o8 + 8].rearrange("a p n -> p a n"), key="c", wr=[R_c["c"]])
        ph.op("dve", lambda e: e.tensor_copy(out=wa_b[:], in_=wa_f[:]), rd=[R_c["c"]], wr=[R_c["c"]])
        ph.op("dve", lambda e: e.tensor_copy(out=wx_b[:], in_=wx_f[:]), rd=[R_c["c"]], wr=[R_c["c"]])
        ph.op("act", lambda e: e.activation(out=lam[:], in_=lam[:], func=AF.Exp, scale=-1.0), rd=[R_c["c"]], wr=[R_c["c"]], strict=True)
        ph.op("dve", lambda e: e.tensor_scalar(out=cl1[:], in0=lam[:], scalar1=-0.25, scalar2=1.0 / 3.0, op0=ALU.mult, op1=ALU.add),
              rd=[R_c["c"]], wr=[R_c["c"]], strict=True)
        ph.op("dve", lambda e: e.tensor_tensor(out=cl1[:], in0=cl1[:], in1=lam[:], op=ALU.mult), rd=[R_c["c"]], wr=[R_c["c"]], strict=True)
        ph.op("dve", lambda e: e.tensor_scalar(out=cl1[:], in0=cl1[:], scalar1=1.0, scalar2=-0.5, op0=ALU.mult, op1=ALU.add), rd=[R_c["c"]], wr=[R_c["c"]], strict=True)
        ph.op("dve", lambda e: e.tensor_tensor(out=cl1[:], in0=cl1[:], in1=lam[:], op=ALU.mult), rd=[R_c["c"]], wr=[R_c["c"]], strict=True)
        ph.op("dve", lambda e: e.tensor_scalar(out=cl1[:], in0=cl1[:], scalar1=1.0, scalar2=1.0, op0=ALU.mult, op1=ALU.add), rd=[R_c["c"]], wr=[R_c["c"]], strict=True)
        ph.op("dve", lambda e: e.tensor_tensor(out=cl1[:], in0=cl1[:], in1=lam[:], op=ALU.mult), rd=[R_c["c"]], wr=[R_c["c"]], strict=True)
        ph.op("dve", lambda e: e.tensor_scalar(out=cl2[:], in0=cl1[:], scalar1=-16.0, scalar2=None, op0=ALU.mult), rd=[R_c["c"]], wr=[R_c["c"]], strict=True)
        ph.op("dve", lambda e: e.tensor_scalar(out=cl1[:], in0=cl1[:], scalar1=-8.0, scalar2=None, op0=ALU.mult), rd=[R_c["c"]], wr=[R_c["c"]], strict=True)
        cnts = {"nx": 0, "nh": 0, "npr": 0}

        def lru_iter(d, cc, ix, tix, tt):
            par = nit[0] % 2
            nit[0] += 1
            xc, xcb, rg, ig, av, a2, uu, hf, grt = (t_[par] for t_ in (xc_2, xcb_2, rg_2, ig_2, av_2, a2_2, uu_2, hf_2, grt_2))
            R = dict(R_2[par])
            R.update(R_c)
            nx, nh, npr = cnts["nx"], cnts["nh"], cnts["npr"]
            if True:
                if True:
                    t0 = tt * TTL
                    xi = nx % 2
                    nx += 1
                    lo = t0 if d == 0 else t0 + 3
                    ph.dma("sp", xin[xi][:], RIN[cc * 128:(cc + 1) * 128, lo:lo + TTL + 3], key=f"x{xi}", wr=[r_xin[xi]])
                    for i in range(4):
                        sh = i if d == 0 else 3 - i
                        wsc = cw[:, ix * 4 + i:ix * 4 + i + 1]
                        if i == 0:
                            ph.op("dve", lambda e, xi=xi, sh=sh, wsc=wsc, ix=ix: e.tensor_scalar(
                                out=xc[:], in0=xin[xi][:, sh:sh + TTL], scalar1=wsc, scalar2=cb[:, ix:ix + 1], op0=ALU.mult, op1=ALU.add),
                                rd=[r_xin[xi], R["c"]], wr=[R["xc"]])
                        else:
                            ph.op("dve", lambda e, xi=xi, sh=sh, wsc=wsc: e.scalar_tensor_tensor(
                                out=xc[:], in0=xin[xi][:, sh:sh + TTL], scalar=wsc, in1=xc[:], op0=ALU.mult, op1=ALU.add),
                                rd=[r_xin[xi], R["c"], R["xc"]], wr=[R["xc"]])
                    ph.op("pool", lambda e: e.tensor_copy(out=xcb[:], in_=xc[:]), rd=[R["xc"]], wr=[R["xcb"]])
                    for sub in range(TTL // 512):
                        b = npr % 2
                        npr += 1
                        sl = slice(sub * 512, (sub + 1) * 512)
                        ph.op("pe", lambda e, b=b, ix=ix, sl=sl: e.matmul(p_r[b][:], lhsT=wa_b[:, ix, :], rhs=xcb[:, sl], start=True, stop=True),
                              rd=[R["xcb"], R["c"]], wr=[r_pr[b]])
                        ph.op("pe", lambda e, b=b, ix=ix, sl=sl: e.matmul(p_i[b][:], lhsT=wx_b[:, ix, :], rhs=xcb[:, sl], start=True, stop=True),
                              rd=[R["xcb"], R["c"]], wr=[r_pi[b]])
                        ph.op("act", lambda e, b=b, ix=ix, sl=sl: e.activation(out=rg[:, sl], in_=p_r[b][:], func=AF.Sigmoid,
                                                                               bias=ba[:, ix:ix + 1], scale=1.0), rd=[r_pr[b], R["c"]], wr=[R["rg"]])
                        ph.op("act", lambda e, b=b, ix=ix, sl=sl: e.activation(out=ig[:, sl], in_=p_i[b][:], func=AF.Sigmoid,
                                                                               bias=bx[:, ix:ix + 1], scale=1.0), rd=[r_pi[b], R["c"]], wr=[R["ig"]])
                    def series(dst, clt, ix=ix):
                        ph.op("dve", lambda e: e.tensor_scalar(out=dst[:], in0=rg[:], scalar1=clt[:, ix:ix + 1], scalar2=None, op0=ALU.mult),
                              rd=[R["rg"], R["c"]], wr=[R["av"], R["a2"]])
                        ph.op("dve", lambda e: e.tensor_scalar(out=uu[:], in0=dst[:], scalar1=0.2, scalar2=1.0, op0=ALU.mult, op1=ALU.add),
                              rd=[R["av"], R["a2"]], wr=[R["uu"]])
                        for cf in (0.25, 1.0 / 3.0, 0.5):
                            ph.op("dve", lambda e: e.tensor_tensor(out=uu[:], in0=uu[:], in1=dst[:], op=ALU.mult), rd=[R["uu"]], wr=[R["uu"]])
                            ph.op("dve", lambda e, cf=cf: e.tensor_scalar(out=uu[:], in0=uu[:], scalar1=cf, scalar2=1.0, op0=ALU.mult, op1=ALU.add),
                                  rd=[R["uu"]], wr=[R["uu"]])
                    series(av, cl1)
                    ph.op("dve", lambda e: e.tensor_tensor(out=a2[:], in0=av[:], in1=uu[:], op=ALU.mult), rd=[R["uu"], R["av"]], wr=[R["a2"]])
                    ph.op("dve", lambda e: e.tensor_scalar(out=av[:], in0=a2[:], scalar1=1.0, scalar2=1.0, op0=ALU.mult, op1=ALU.add),
                          rd=[R["a2"]], wr=[R["av"]])
                    ph.op("dve", lambda e: e.tensor_scalar(out=uu[:], in0=av[:], scalar1=1.0, scalar2=1.0, op0=ALU.mult, op1=ALU.add),
                          rd=[R["av"]], wr=[R["uu"]])
                    ph.op("dve", lambda e: e.scalar_tensor_tensor(out=a2[:], in0=a2[:], scalar=-1.0, in1=uu[:], op0=ALU.mult, op1=ALU.mult),
                          rd=[R["uu"], R["a2"]], wr=[R["a2"]])
                    ph.op("act", lambda e: e.activation(out=a2[:], in_=a2[:], func=AF.Sqrt), rd=[R["a2"]], wr=[R["a2"]])
                    ph.op("pool", lambda e: e.tensor_tensor(out=uu[:], in0=ig[:], in1=xc[:], op=ALU.mult), rd=[R["ig"], R["xc"]], wr=[R["uu"]])
                    ph.op("pool", lambda e: e.tensor_tensor(out=uu[:], in0=uu[:], in1=a2[:], op=ALU.mult), rd=[R["uu"], R["a2"]], wr=[R["uu"]])
                    if DBGT is not None and d == 0 and cc == 0 and tix == 0 and l == 0:
                        for di, (tn, rn) in enumerate(((xc, "xc"), (rg, "rg"), (ig, "ig"), (av, "av"), (a2, "a2"), (uu, "uu"))):
                            ph.dma("sp", DBGT[di], tn[:], key="dbg", rd=[R[rn]])
                        ph.dma("sp", DBGC[:, 0:8], cl1[:], key="dbg", rd=[R["c"]])
                        ph.dma("sp", DBGC[:, 8:16], cl2[:], key="dbg", rd=[R["c"]])
                        ph.dma("sp", DBGC[:, 16:24], lam[:], key="dbg", rd=[R["c"]])
                        ph.dma("sp", DBGC[:, 24:32], cb[:], key="dbg", rd=[R["c"]])
                        ph.dma("sp", DBGC[:, 32:40], ba[:], key="dbg", rd=[R["c"]])
                        ph.dma("sp", DBGC[:, 40:48], bx[:], key="dbg", rd=[R["c"]])
                    hi = nh % 2
                    nh += 1
                    init = 0.0 if tix == 0 else carry[:, 0:1]
                    if d == 0:
                        ph.op("dve", lambda e, hi=hi, init=init: e.tensor_tensor_scan(out=hh[hi][:], data0=av[:], data1=uu[:], initial=init,
                                                                                      op0=ALU.mult, op1=ALU.add),
                              rd=[R["av"], R["uu"], R["carry"]], wr=[r_hh[hi]], strict=True)
                        ph.op("dve", lambda e, hi=hi: e.tensor_copy(out=carry[:], in_=hh[hi][:, TTL - 1:TTL]), rd=[r_hh[hi]], wr=[R["carry"]], strict=True)
                        ph.dma("sp", HF[cc * 128:(cc + 1) * 128, t0:t0 + TTL], hh[hi][:], key=f"h{hi}", rd=[r_hh[hi]])
                    else:
                        ph.dma("sp", hf[:], HF[cc * 128:(cc + 1) * 128, t0:t0 + TTL], key=f"hf{par}", wr=[R["hf"]])
                        ph.dma("sp", grt[:], GR[cc * 128:(cc + 1) * 128, t0:t0 + TTL], key=f"gr{par}", wr=[R["grt"]])
                        ph.op("dve", lambda e, hi=hi, init=init: e.tensor_tensor_scan(out=hh[hi][:, ::-1], data0=av[:, ::-1], data1=uu[:, ::-1],
                                                                                      initial=init, op0=ALU.mult, op1=ALU.add),
                              rd=[R["av"], R["uu"], R["carry"]], wr=[r_hh[hi]], strict=True)
                        ph.op("dve", lambda e, hi=hi: e.tensor_copy(out=carry[:], in_=hh[hi][:, 0:1]), rd=[r_hh[hi]], wr=[R["carry"]], strict=True)
                        ph.op("pool", lambda e, hi=hi: e.tensor_tensor(out=hf[:], in0=hf[:], in1=hh[hi][:], op=ALU.add),
                              rd=[r_hh[hi], R["hf"]], wr=[R["hf"]])
                        ph.op("pool", lambda e, hi=hi: e.tensor_tensor(out=yr[hi][:], in0=hf[:], in1=grt[:], op=ALU.mult),
                              rd=[R["hf"], R["grt"]], wr=[r_yr[hi]])
                        ph.dma("sp", Y[1024 + cc * 128:1024 + (cc + 1) * 128, t0:t0 + TTL], yr[hi][:], key=f"h{hi}", rd=[r_yr[hi]])
            cnts["nx"], cnts["nh"], cnts["npr"] = nx, nh, npr

        for d in range(2):
            for cc in range(4):
                ix = d * 4 + cc
                tiles = list(range(NTTL)) if d == 0 else list(range(NTTL - 1, -1, -1))
                for tix, tt in enumerate(tiles):
                    lru_iter(d, cc, ix, tix, tt)
        ph.run()

        ph = Phase(nc, f"cv_{l}")
        w31 = ph.sb("w31", [128, 4 * 31], F32)
        b31 = ph.sb("b31", [128, 4], F32)
        lg = ph.sb("lg", [128, 4], F32)
        lb = ph.sb("lb", [128, 4], F32)
        ones = ph.sb("ones", [128, 128], F32)
        epst = ph.sb("eps", [128, 1], F32)
        uin = [ph.sb(f"uin{i}", [128, TT + 30], F32) for i in range(2)]
        ub = [ph.sb(f"ub{i}", [128, TT + 30], BF16) for i in range(2)]
        dg_f = ph.sb("dg_f", [128, 31, 128], F32)
        dg = ph.sb("dg", [128, 4, 31, 128], BF16)
        acc = ph.sb("acc", [128, 4, TT], F32)
        sqc = [ph.sb(f"sqc{i}", [128, 512], F32) for i in range(2)]
        mean = ph.sb("mean", [128, 512], F32)
        var = ph.sb("var", [128, 512], F32)
        tmpc = [ph.sb(f"tmpc{i}", [128, 512], F32) for i in range(2)]
        yc = [ph.sb(f"yc{i}", [128, 4, 512], BF16) for i in range(2)]
        p_s = ph.ps("p_s")
        p_q = ph.ps("p_q")
        pc = [ph.ps(f"pc{i}") for i in range(2)]
        R = {n: Res() for n in ["c", "mean", "var", "ps", "pq", "dgf", "dg"]}
        r_uin, r_ub, r_sqc, r_tmpc, r_yc, r_pc = ([Res(), Res()] for _ in range(6))
        r_acc = [[Res(), Res()] for _ in range(4)]
        for tns, src in ((b31, cfb), (lg, clg), (lb, clb)):
            ph.dma("sp", tns[:], src[:, l * 4:(l + 1) * 4], key="c", wr=[R["c"]])
        for cc in range(4):
            ph.dma("sp", dg_f[:], cfd[l * 4 + cc], key="dgf", wr=[R["dgf"]])
            ph.op("dve", lambda e, cc=cc: e.tensor_copy(out=dg[:, cc, :, :], in_=dg_f[:]), rd=[R["dgf"]], wr=[R["dg"]])
        ph.op("dve", lambda e: e.memset(ones[:], 1.0), wr=[R["c"]])
        ph.op("dve", lambda e: e.memset(epst[:], EPS), wr=[R["c"]])
        nu = 0
        nq = 0
        ny = 0
        npc = 0
        H2 = TT // 2
        for tt in range(NTT):
            t0 = tt * TT
            for cc in range(4):
                ui = nu % 2
                nu += 1
                ph.dma("sp", uin[ui][:], UC[cc * 128:(cc + 1) * 128, t0:t0 + TT + 30], key=f"u{ui}", wr=[r_uin[ui]])
                ph.op("pool", lambda e, ui=ui: e.tensor_copy(out=ub[ui][:], in_=uin[ui][:]), rd=[r_uin[ui]], wr=[r_ub[ui]])
                for sub in range(TT // 512):
                    b = npc % 2
                    npc += 1
                    half = (sub * 512) // H2
                    for i in range(31):
                        ph.op("pe", lambda e, b=b, cc=cc, i=i, ui=ui, sub=sub: e.matmul(
                            pc[b][:], lhsT=dg[:, cc, i, :], rhs=ub[ui][:, sub * 512 + i:sub * 512 + i + 512], start=(i == 0), stop=(i == 30)),
                            rd=[R["dg"], r_ub[ui]], wr=[r_pc[b]])
                    ph.op("act", lambda e, b=b, cc=cc, sub=sub: e.activation(
                        out=acc[:, cc, sub * 512:(sub + 1) * 512], in_=pc[b][:], func=AF.Identity, bias=b31[:, cc:cc + 1], scale=1.0),
                        rd=[r_pc[b], R["c"]], wr=[r_acc[cc][half]])
            for sub in range(TT // 512):
                sl = slice(sub * 512, (sub + 1) * 512)
                half = (sub * 512) // H2
                for cc in range(4):
                    ph.op("pe", lambda e, cc=cc, sl=sl: e.matmul(p_s[:], lhsT=ones[:], rhs=acc[:, cc, sl], start=(cc == 0), stop=(cc == 3)),
                          rd=[r_acc[cc][half], R["c"]], wr=[R["ps"]])
                for cc in range(4):
                    qi = nq % 2
                    nq += 1
                    ph.op("act", lambda e, cc=cc, sl=sl, qi=qi: e.activation(out=sqc[qi][:], in_=acc[:, cc, sl], func=AF.Square),
                          rd=[r_acc[cc][half]], wr=[r_sqc[qi]])
                    ph.op("pe", lambda e, cc=cc, qi=qi: e.matmul(p_q[:], lhsT=ones[:], rhs=sqc[qi][:], start=(cc == 0), stop=(cc == 3)),
                          rd=[r_sqc[qi], R["c"]], wr=[R["pq"]])
                ph.op("dve", lambda e: e.tensor_scalar(out=mean[:], in0=p_s[:], scalar1=1.0 / 512.0, scalar2=None, op0=ALU.mult),
                      rd=[R["ps"]], wr=[R["mean"]])
                ph.op("dve", lambda e: e.tensor_tensor(out=var[:], in0=mean[:], in1=mean[:], op=ALU.mult), rd=[R["mean"]], wr=[R["var"]])
                ph.op("dve", lambda e: e.scalar_tensor_tensor(out=var[:], in0=p_q[:], scalar=1.0 / 512.0, in1=var[:], op0=ALU.mult,
                                                              op1=ALU.subtract), rd=[R["pq"], R["var"]], wr=[R["var"]])
                ph.op("act", lambda e: e.activation(out=var[:], in_=var[:], func=AF.Sqrt, bias=epst[:]), rd=[R["var"], R["c"]], wr=[R["var"]])
                ph.op("dve", lambda e: e.reciprocal(out=var[:], in_=var[:]), rd=[R["var"]], wr=[R["var"]])
                yi = ny % 2
                ny += 1
                for cc in range(4):
                    b = cc % 2
                    ph.op("dve", lambda e, cc=cc, sl=sl, b=b: e.tensor_tensor(out=tmpc[b][:], in0=acc[:, cc, sl], in1=mean[:], op=ALU.subtract),
                          rd=[r_acc[cc][half], R["mean"]], wr=[r_tmpc[b]])
                    ph.op("pool", lambda e, b=b: e.tensor_tensor(out=tmpc[b][:], in0=tmpc[b][:], in1=var[:], op=ALU.mult),
                          rd=[r_tmpc[b], R["var"]], wr=[r_tmpc[b]])
                    ph.op("act", lambda e, cc=cc, b=b, yi=yi: e.activation(out=yc[yi][:, cc, :], in_=tmpc[b][:], func=AF.Silu,
                                                                           scale=lg[:, cc:cc + 1], bias=lb[:, cc:cc + 1]),
                          rd=[r_tmpc[b], R["c"]], wr=[r_yc[yi]])
                c0 = t0 + sub * 512
                ph.dma("sp", Y[1536:2048, c0:c0 + 512].rearrange("(c p) t -> p c t", p=128), yc[yi][:], key=f"y{yi}", rd=[r_yc[yi]])
        ph.run()

        ph = Phase(nc, f"f2_{l}")
        tc = TileCtx(ph, with_xT=False)
        ffn = FFN(ph, tc)
        ytile = ffn.hT
        cnt = [0]
        x_dst = xB if l < L - 1 else yT_out
        r_xC = Res()
        for ti in range(NT):
            t0 = ti * T
            ph.dma("sp", ytile[:, 0:KC, :], Y[:, t0:t0 + T].rearrange("(c p) t -> p c t", p=128), key="yt",
                   wr=[ffn.r_h[i] for i in range(KC)])
            for dc in range(KC):
                s = ffn.nout % 2
                ffn.nout += 1
                ph.dma("sp", ffn.wout[s][:, 0:KC, :], wmo_b[l, dc], key=f"wout{s}", wr=[ffn.r_wout[s]])
                for kc in range(KC):
                    ph.op("pe", lambda e, s=s, kc=kc: e.matmul(ffn.po[s][:], lhsT=ffn.wout[s][:, kc, :], rhs=ytile[:, kc, :],
                                                               start=(kc == 0), stop=(kc == KC - 1)),
                          rd=[ffn.r_wout[s], ffn.r_h[kc]], wr=[ffn.r_po[s]])
                ph.op("act", lambda e, s=s, dc=dc: e.activation(out=tc.xo[:, dc, :], in_=ffn.po[s][:], func=AF.Copy),
                      rd=[ffn.r_po[s]], wr=[tc.r_xo])
            post_stream(ph, tc, l, 1, cnt, xA, t0)
            store_xo(ph, tc, xC, t0, dres=r_xC)
            pre_norm(ph, tc, ffn.xn, ffn.r_xn, l, 2, cnt, src=tc.xo, src_res=[tc.r_xo] * KC)
            ffn.tile_in(l, 1)
            ffn.tile_out(l, 1)
            post_stream(ph, tc, l, 2, cnt, xC, t0, dres=r_xC)
            store_xo(ph, tc, x_dst, t0)
        ph.run()


def _consts(S):
    N2 = S // 128
    CHP = 128 // N2
    c = {}
    c["c_ident"] = np.eye(128, dtype=np.float32)
    s = np.arange(128)[:, None]
    t = np.arange(128)[None, :]
    v = np.float32(-1.0 / 16.0)
    trif = np.where(s <= t, v, 0).astype(np.float32)
    remf = np.where(s > t, v, 0).astype(np.float32)
    trib = np.where(s >= t, v, 0).astype(np.float32)
    remb = np.where(s < t, v, 0).astype(np.float32)
    c["c_tri"] = np.stack([trif, remf, trib, remb])
    c["c_mask"] = np.stack([(s <= t), (s >= t)]).astype(np.float32)
    ang = 2 * np.pi * (s * t % 128) / 128.0
    C, Sn = np.cos(ang), np.sin(ang)
    c["c_cs128"] = np.concatenate([C, -Sn], 1).astype(np.float32)
    c["c_sa"] = np.stack([np.concatenate([C, -Sn], 1), np.concatenate([Sn, C], 1)]).astype(np.float32)
    m = np.arange(128)
    s2 = (m % N2)[:, None]
    k1 = np.arange(128)[None, :]
    th = 2 * np.pi * (s2 * k1 % S) / S
    c["c_tw"] = np.stack([np.cos(th), -np.sin(th)]).astype(np.float32)
    j_r = (m // N2)[:, None]
    j_c = (m // N2)[None, :]
    a2 = (m % N2)[:, None]
    b2 = (m % N2)[None, :]
    th2 = 2 * np.pi * (a2 * b2 % N2) / N2
    same = (j_r == j_c)
    c["c_bd"] = np.stack([np.where(same, np.cos(th2), 0), np.where(same, np.sin(th2), 0)]).astype(np.float32)
    return c


def _pp(v, n):
    v = np.asarray(v, np.float32)
    lead = v.shape[:-1]
    return np.ascontiguousarray(np.moveaxis(v.reshape(lead + (n, 128)), -1, 0)).reshape(128, -1)


def prep_shared(inp, S):
    f = lambda a: np.ascontiguousarray(np.asarray(a, dtype=np.float32))
    g = {}
    w_ada = f(inp["w_ada"])
    g["wada"] = np.ascontiguousarray(w_ada.reshape(L, KC, 128, 36, 512).transpose(0, 3, 2, 1, 4))
    g["bada"] = _pp(f(inp["b_ada"]), 144)
    g["gpre"] = _pp(f(inp["g_pre"]), 16)
    g["gpost"] = _pp(f(inp["g_post"]), 16)
    for nm, key in (("w1", "ffn1"), ("w2", "ffn2")):
        w_in = f(inp[key + "_w_in"])
        DFF = w_in.shape[2] // 2
        HCn = DFF // 128
        up = w_in[:, :, :DFF].reshape(L, KC, 128, HCn, 128)
        gt = w_in[:, :, DFF:].reshape(L, KC, 128, HCn, 128)
        g[nm + "in"] = np.ascontiguousarray(np.concatenate([up, gt], -1).transpose(0, 3, 2, 1, 4))
        w_out = f(inp[key + "_w_out"])
        HC = DFF // 128
        g[nm + "out"] = np.ascontiguousarray(w_out.reshape(L, HC, 128, KC, 128).transpose(0, 3, 2, 1, 4))
    wm = f(inp["w_mix_in"])
    o = np.cumsum([0, 512, 256, 256, 512, 512, 32, 512, 512, 1024])
    fcol, qcol, kcol, vcol, ogcol, acol, rincol, rgcol, ccol = [np.arange(o[i], o[i + 1]) for i in range(9)]
    cv, cg = ccol[:512], ccol[512:]
    cinter = np.concatenate([np.concatenate([cg[i * 128:(i + 1) * 128], cv[i * 128:(i + 1) * 128]]) for i in range(4)])
    order = np.concatenate([fcol, qcol, kcol, ogcol, rincol, rgcol, cinter])
    wfm = np.zeros((L, D, NCH_FM * 128), np.float32)
    wfm[:, :, :28 * 128] = wm[:, :, order]
    wfm[:, :, 28 * 128:28 * 128 + 16] = wm[:, :, acol[:16]]
    wfm[:, :, 28 * 128 + 32:28 * 128 + 48] = wm[:, :, acol[16:]]
    g["wmi"] = np.ascontiguousarray(wfm.reshape(L, KC, 128, NCH_FM, 128).transpose(0, 3, 2, 1, 4))
    g["wmt"] = np.ascontiguousarray(wm[:, :, np.concatenate([kcol, vcol])].reshape(L, KC, 128, 768).transpose(0, 2, 1, 3))
    g["wmo"] = np.ascontiguousarray(f(inp["w_mix_out"]).reshape(L, KC, 128, KC, 128).transpose(0, 3, 2, 1, 4))
    wal = f(inp["gla_w_alpha"])
    wal64 = np.zeros((64, L * 2 * 256), np.float32)
    for l in range(L):
        for d in range(2):
            wal64[32 * d:32 * d + 16, (l * 2 + d) * 256:(l * 2 + d + 1) * 256] = wal[l, d]
    g["walpha"] = wal64
    bal = f(inp["gla_b_alpha"])
    bal64 = np.zeros((64, L * 2 * 256), np.float32)
    for l in range(L):
        for d in range(2):
            bal64[32 * d, (l * 2 + d) * 256:(l * 2 + d + 1) * 256] = bal[l, d]
    g["balpha"] = bal64
    g["gnorm"] = np.ascontiguousarray(f(inp["gla_norm_g"]).T)
    lcw = f(inp["lru_conv_w"])
    g["lcw"] = np.ascontiguousarray(lcw.reshape(L, 2, 4, 4, 128).transpose(4, 0, 1, 3, 2)).reshape(128, -1)
    for nm, key in (("lcb", "lru_conv_b"), ("lba", "lru_b_a"), ("lbx", "lru_b_x"), ("llam", "lru_lambda")):
        g[nm] = _pp(f(inp[key]), 4)
    for nm, key in (("lwa", "lru_w_a"), ("lwx", "lru_w_x")):
        w = f(inp[key])
        m = np.zeros((L, 2, 4, 128, 128), np.float32)
        for cc in range(4):
            m[:, :, cc, 0:64, 0:64] = w[:, :, 2 * cc]
            m[:, :, cc, 64:128, 64:128] = w[:, :, 2 * cc + 1]
        g[nm] = m.reshape(L * 2 * 4, 128, 128)
    cw = f(inp["conf_dw_w"])
    g["cfw"] = np.ascontiguousarray(cw.reshape(L, 31, 4, 128).transpose(3, 0, 2, 1)).reshape(128, -1)
    cfd = np.zeros((L, 4, 128, 31, 128), np.float32)
    idx = np.arange(128)
    cfd[:, :, idx, :, idx] = cw.reshape(L, 31, 4, 128).transpose(3, 0, 2, 1)
    g["cfd"] = cfd.reshape(L * 4, 128, 31, 128)
    g["cfb"] = _pp(f(inp["conf_dw_b"]), 4)
    g["clg"] = _pp(f(inp["conf_ln_g"]), 4)
    g["clb"] = _pp(f(inp["conf_ln_b"]), 4)
    g.update(_consts(S))
    return g


_NC_CACHE = {}


def run_seqs(xs, cs, inp):
    S = xs[0].shape[0]
    DFF = inp["ffn1_w_in"].shape[2] // 2
    key = (S, DFF)
    if key not in _NC_CACHE:
        _NC_CACHE[key] = build_program(S, DFF)
    nc = _NC_CACHE[key]
    shared = prep_shared(inp, S)
    n = DBG.get("ncores", 8)
    in_maps = []
    for i in range(n):
        j = i if i < len(xs) else 0
        m = dict(shared)
        m["xT"] = np.ascontiguousarray(np.asarray(xs[j], np.float32).T)
        m["cT"] = np.ascontiguousarray(np.asarray(cs[j], np.float32).reshape(KC, 128).T)
        in_maps.append(m)
    res = run_bass_kernel_spmd(nc, in_maps, core_ids=list(range(n)))
    if DBG["outs"]:
        DBG["res"] = res.results
    return [np.ascontiguousarray(res.results[i]["yT"].T) for i in range(len(xs))]


def kernel(**inp):
    xp = np.asarray(inp["x_prompt"], np.float32)
    xs_ = np.asarray(inp["x_sample"], np.float32)
    cp = np.asarray(inp["c_prompt"], np.float32)
    cs_ = np.asarray(inp["c_sample"], np.float32)
    xs = [xp[b] for b in range(xp.shape[0])] + [xs_[b] for b in range(xs_.shape[0])]
    cs = [cp[b] for b in range(cp.shape[0])] + [cs_[b] for b in range(cs_.shape[0])]
    outs = run_seqs(xs, cs, inp)
    nb = xp.shape[0]
    y_p = np.stack(outs[:nb]).astype(np.float32)
    y_s = np.stack(outs[nb:]).astype(np.float32)
    return (y_p, y_s)
```
